# Optimizing a Trainium2 kernel written in Bass

```python
import math
import jax, jax.numpy as jnp
from jax import lax
import numpy as np

D_MODEL = 1024
BATCH = 4
SEQ = 8192
DEPTH = 2

HEAD_DIM = 64
N_SLOTS = 8
DILATED_GROUPS = ((128, 1), (512, 4), (2048, 16))
N_GROUPS = len(DILATED_GROUPS)
HALF_SPAN = 64
BLK = 64
ATT_WIDTH = N_SLOTS * HEAD_DIM
QKV_WIDTH = N_GROUPS * 3 * ATT_WIDTH
ALIBI_MAX = 8.0
POOL_WINDOWS = (2, 4, 8, 16)
POOL_GROUP = 128
POOL_WIDTH = len(POOL_WINDOWS) * POOL_GROUP
IN_WIDTH = QKV_WIDTH + POOL_WIDTH + 2 * D_MODEL
D_FF = -(-8 * D_MODEL // (3 * 256)) * 256
ALPHA = (2 * DEPTH) ** 0.25
BETA = (8 * DEPTH) ** -0.25
LN_EPS = 1e-5
N_MOD = 6

kernel_name = "hybrid_dilated_attn_pool_deepnorm_adaln"


def layer_norm(x, g=None, b=None):
    xf = x.astype(jnp.float32)
    mu = jnp.mean(xf, axis=-1, keepdims=True)
    var = jnp.mean(jnp.square(xf - mu), axis=-1, keepdims=True)
    y = (xf - mu) * lax.rsqrt(var + LN_EPS)
    if g is not None:
        y = y * g.astype(jnp.float32) + b.astype(jnp.float32)
    return y.astype(x.dtype)


def alibi_slopes():
    n = N_GROUPS * N_SLOTS
    i = jnp.arange(1, n + 1, dtype=jnp.float32)
    return jnp.exp2(-ALIBI_MAX * i / n).reshape(N_GROUPS, N_SLOTS)


def dilated_band_attention(q, k, v, rate, slopes):
    B, S, H, Dh = q.shape
    L = S // rate
    nb = -(-L // BLK)
    Lp = nb * BLK

    def phases(t):
        t = t.reshape(B, L, rate, H, Dh).transpose(0, 2, 3, 1, 4).reshape(B * rate, H, L, Dh)
        return jnp.pad(t, ((0, 0), (0, 0), (0, Lp - L), (0, 0)))

    def windows(t):
        t = jnp.pad(t, ((0, 0), (0, 0), (BLK, BLK), (0, 0))).reshape(B * rate, H, nb + 2, BLK, Dh)
        return jnp.concatenate([t[:, :, :-2], t[:, :, 1:-1], t[:, :, 2:]], axis=3)

    qb = phases(q).reshape(B * rate, H, nb, BLK, Dh)
    kw = windows(phases(k))
    vw = windows(phases(v))

    scores = jnp.einsum('bhnqd,bhnkd->bhnqk', qb, kw).astype(jnp.float32) * (Dh ** -0.5)
    a = jnp.arange(BLK)[:, None]
    cidx = jnp.arange(3 * BLK)[None, :]
    rel = cidx - BLK - a
    kpos = (jnp.arange(nb)[:, None] - 1) * BLK + jnp.arange(3 * BLK)[None, :]
    valid = (jnp.abs(rel) <= HALF_SPAN)[None] & ((kpos >= 0) & (kpos < L))[:, None, :]
    dist = (rate * jnp.abs(rel)).astype(jnp.float32)
    scores = scores - slopes.astype(jnp.float32)[:, None, None, None] * dist[None, None]
    scores = jnp.where(valid, scores, -1e30)
    lse = jax.nn.logsumexp(scores, axis=-1)
    p = jnp.exp(scores - lse[..., None])
    out = jnp.einsum('bhnqk,bhnkd->bhnqd', p.astype(vw.dtype), vw)

    out = out.reshape(B * rate, H, Lp, Dh)[:, :, :L]
    out = out.reshape(B, rate, H, L, Dh).transpose(0, 3, 1, 2, 4).reshape(B, S, H, Dh)
    lse = lse.reshape(B * rate, H, Lp)[:, :, :L]
    lse = lse.reshape(B, rate, H, L).transpose(0, 3, 1, 2).reshape(B, S, H)
    return out, lse


def centred_pool_minus_identity(u, window):
    B, S, C = u.shape
    cs = jnp.concatenate([jnp.zeros((B, 1, C), jnp.float32),
                          jnp.cumsum(u.astype(jnp.float32), axis=1)], axis=1)
    t = jnp.arange(S)
    lo = jnp.clip(t - window // 2, 0, S)
    hi = jnp.clip(t + window // 2, 0, S)
    mean = (cs[:, hi] - cs[:, lo]) / (hi - lo).astype(jnp.float32)[None, :, None]
    return (mean - u.astype(jnp.float32)).astype(u.dtype)


def setup_inputs(seed: int = 0) -> dict:
    key = jax.random.key(seed)
    ks = jax.random.split(key, 20)
    f32 = jnp.float32
    nrm = lambda k, shape, s: jax.random.normal(k, shape, f32) * s
    L = DEPTH
    return {
        "x": nrm(ks[0], (BATCH, SEQ, D_MODEL), 1.0),
        "c": nrm(ks[1], (BATCH, D_MODEL), 1.0),
        "w_ada": nrm(ks[2], (L, D_MODEL, N_MOD * D_MODEL), 0.2 * D_MODEL ** -0.5),
        "b_ada": nrm(ks[3], (L, N_MOD * D_MODEL), 0.01),
        "w_in": nrm(ks[4], (L, D_MODEL, IN_WIDTH), D_MODEL ** -0.5),
        "w_pool_mix": nrm(ks[5], (L, len(POOL_WINDOWS), POOL_GROUP, POOL_GROUP), POOL_GROUP ** -0.5),
        "pool_scale": 1.0 + nrm(ks[6], (L, POOL_WIDTH), 0.1),
        "w_att_out": nrm(ks[7], (L, ATT_WIDTH, D_MODEL), ATT_WIDTH ** -0.5),
        "w_pool_out": nrm(ks[8], (L, POOL_WIDTH, D_MODEL), POOL_WIDTH ** -0.5),
        "w_o": nrm(ks[9], (L, D_MODEL, D_MODEL), BETA * D_MODEL ** -0.5),
        "ln1_g": 1.0 + nrm(ks[10], (L, D_MODEL), 0.05),
        "ln1_b": nrm(ks[11], (L, D_MODEL), 0.02),
        "w_ffn_in": nrm(ks[12], (L, D_MODEL, 2 * D_FF), D_MODEL ** -0.5),
        "w_ffn_out": nrm(ks[13], (L, D_FF, D_MODEL), BETA * D_FF ** -0.5),
        "ln2_g": 1.0 + nrm(ks[14], (L, D_MODEL), 0.05),
        "ln2_b": nrm(ks[15], (L, D_MODEL), 0.02),
    }


def reference(x, c, w_ada, b_ada, w_in, w_pool_mix, pool_scale, w_att_out, w_pool_out, w_o,
              ln1_g, ln1_b, w_ffn_in, w_ffn_out, ln2_g, ln2_b):
    B, S, D = x.shape
    slopes = alibi_slopes()
    for l in range(DEPTH):
        mod = jax.nn.silu(c) @ w_ada[l] + b_ada[l]
        shift1, scale1, gate1, shift2, scale2, gate2 = [m[:, None, :] for m in jnp.split(mod, N_MOD, axis=-1)]

        h = layer_norm(x) * (1.0 + scale1) + shift1
        proj = h @ w_in[l]
        qkv = proj[..., :QKV_WIDTH].reshape(B, S, N_GROUPS, 3, N_SLOTS, HEAD_DIM)
        pool_in = proj[..., QKV_WIDTH:QKV_WIDTH + POOL_WIDTH]
        gate_att = proj[..., QKV_WIDTH + POOL_WIDTH:QKV_WIDTH + POOL_WIDTH + D]
        gate_pool = proj[..., QKV_WIDTH + POOL_WIDTH + D:]

        outs, lses = [], []
        for g, (_, rate) in enumerate(DILATED_GROUPS):
            o, s = dilated_band_attention(qkv[:, :, g, 0], qkv[:, :, g, 1], qkv[:, :, g, 2], rate, slopes[g])
            outs.append(o)
            lses.append(s)
        wts = jax.nn.softmax(jnp.stack(lses, axis=0), axis=0)
        att = jnp.sum(wts[..., None].astype(x.dtype) * jnp.stack(outs, axis=0), axis=0)
        y_att = att.reshape(B, S, ATT_WIDTH) @ w_att_out[l]

        pu = pool_in.reshape(B, S, len(POOL_WINDOWS), POOL_GROUP)
        pooled = jnp.stack([centred_pool_minus_identity(pu[:, :, i], w) for i, w in enumerate(POOL_WINDOWS)], axis=2)
        pooled = jnp.einsum('bsgc,gcd->bsgd', pooled, w_pool_mix[l]).reshape(B, S, POOL_WIDTH) * pool_scale[l]
        y_pool = pooled @ w_pool_out[l]

        merged = jax.nn.sigmoid(gate_att) * y_att + jax.nn.sigmoid(gate_pool) * y_pool
        mix_out = merged @ w_o[l]
        x = layer_norm(ALPHA * x + (1.0 + gate1) * mix_out, ln1_g[l], ln1_b[l])

        h2 = layer_norm(x) * (1.0 + scale2) + shift2
        a_, b_ = jnp.split(h2 @ w_ffn_in[l], 2, axis=-1)
        ffn_out = (jax.nn.silu(a_) * b_) @ w_ffn_out[l]
        x = layer_norm(ALPHA * x + (1.0 + gate2) * ffn_out, ln2_g[l], ln2_b[l])
    return x
```

```python
import numpy as np
import ml_dtypes
from contextlib import ExitStack
import concourse.bass as bass
import concourse.mybir as mybir
from concourse.bass_utils import run_bass_kernel_spmd

F32 = mybir.dt.float32
BF16 = mybir.dt.bfloat16
AF = mybir.ActivationFunctionType
ALU = mybir.AluOpType

D = 1024
SEQ = 8192
BATCH = 4
DEPTH = 2
TL = 6144
TOWN = 4096
RATES = (1, 4, 16)
PADS = (64, 256, 1024)
DFF = 2816
ALPHA = (2 * DEPTH) ** 0.25
EPS = 1e-5
REGIONS = ((6144, 5120), (5120, 4096))
NEG = -30000.0


class Prog:
    ENG = ("sync", "scalar", "gpsimd", "vector", "tensor")

    def __init__(self):
        self.ops = {e: [] for e in self.ENG}
        self.semcnt = {}
        self.lastw = {}
        self.readers = {}
        self.waited = {e: {} for e in self.ENG}
        self.nops = 0

    def _sem(self, name):
        if name not in self.semcnt:
            self.semcnt[name] = 0
        return name

    def _collect(self, eng, R, W, skip_same_chan=None):
        waits = {}

        def add(t):
            if t is None:
                return
            s, v = t
            if eng == "tensor" and s == "E_tensor":
                return
            if skip_same_chan is not None and s == skip_same_chan:
                return
            if waits.get(s, 0) < v:
                waits[s] = v
        for k in R:
            add(self.lastw.get(k))
        for k in W:
            add(self.lastw.get(k))
            for s, v in self.readers.get(k, {}).items():
                add((s, v))
        out = []
        wd = self.waited[eng]
        for s, v in waits.items():
            if wd.get(s, 0) < v:
                wd[s] = v
                out.append((s, v))
        return out

    def _commit(self, ticket, R, W):
        for k in R:
            d = self.readers.setdefault(k, {})
            if d.get(ticket[0], 0) < ticket[1]:
                d[ticket[0]] = ticket[1]
        for k in W:
            self.lastw[k] = ticket
            self.readers[k] = {}

    def op(self, eng, fn, R=(), W=()):
        sem = self._sem("E_" + eng)
        waits = self._collect(eng, R, W)
        self.semcnt[sem] += 1
        ticket = (sem, self.semcnt[sem])
        self._commit(ticket, R, W)
        self.ops[eng].append((waits, fn, (sem, 1)))
        self.nops += 1
        return ticket

    def pe(self, fn, R=(), W=(), sig=True):
        sem = self._sem("E_tensor")
        waits = self._collect("tensor", R, W)
        ticket = (sem, self.semcnt[sem] + 1)
        if sig:
            self.semcnt[sem] += 1
        self._commit(ticket, R, W)
        self.ops["tensor"].append((waits, fn, (sem, 1) if sig else None))
        self.nops += 1
        return ticket

    def dma(self, eng, chan, out, in_, R=(), W=()):
        sem = self._sem("D_" + chan)
        waits = self._collect(eng, R, W, skip_same_chan=sem)
        self.semcnt[sem] += 16
        ticket = (sem, self.semcnt[sem])
        self._commit(ticket, R, W)
        self.ops[eng].append((waits, lambda e, o=out, i=in_: e.dma_start(out=o, in_=i), (sem, 16)))
        self.nops += 1
        return ticket

    def barrier(self):
        allw = [(s, v) for s, v in self.semcnt.items() if v > 0]
        for e in self.ENG:
            ws = []
            for s, v in allw:
                if e == "tensor" and s == "E_tensor":
                    continue
                if self.waited[e].get(s, 0) < v:
                    self.waited[e][s] = v
                    ws.append((s, v))
            if ws:
                self.ops[e].append((ws, None, None))
        self.lastw = {}
        self.readers = {}

    def emit(self, nc, stack):
        sems = {name: stack.enter_context(nc.semaphore(name)) for name in self.semcnt}
        block = stack.enter_context(nc.Block())

        def run(eng_name):
            def body(e):
                for waits, fn, inc in self.ops[eng_name]:
                    for s, v in waits:
                        e.wait_ge(sems[s], v)
                    if fn is not None:
                        ins = fn(e)
                        if inc is not None:
                            ins.then_inc(sems[inc[0]], inc[1])
            return body
        block.sync(run("sync"))
        block.scalar(run("scalar"))
        block.gpsimd(run("gpsimd"))
        block.vector(run("vector"))
        block.tensor(run("tensor"))


class _Stop(Exception):
    pass


def build_program(debug=False, stop=None):
    nc = bass.Bass("TRN2", target_bir_lowering=False)
    P = Prog()
    dt = nc.dram_tensor

    x_in = dt("x", [TL, D], F32, kind="ExternalInput").ap()
    c_col = dt("c_col", [128, 8], F32, kind="ExternalInput").ap()
    w_ada = dt("w_ada", [DEPTH, D, 6 * D], F32, kind="ExternalInput").ap()
    b_ada = dt("b_ada", [DEPTH, 6 * D], F32, kind="ExternalInput").ap()
    w_in = dt("w_in", [DEPTH, D, 7168], F32, kind="ExternalInput").ap()
    w_mix = dt("w_pool_mix", [DEPTH, 4, 128, 128], F32, kind="ExternalInput").ap()
    psc_in = dt("pool_scale_col", [DEPTH, 128, 4], F32, kind="ExternalInput").ap()
    w_ao = dt("w_att_out", [DEPTH, 512, D], F32, kind="ExternalInput").ap()
    w_po = dt("w_pool_out", [DEPTH, 512, D], F32, kind="ExternalInput").ap()
    w_o = dt("w_o", [DEPTH, D, D], F32, kind="ExternalInput").ap()
    ln1g = dt("ln1_g", [DEPTH, D], F32, kind="ExternalInput").ap()
    ln1b = dt("ln1_b", [DEPTH, D], F32, kind="ExternalInput").ap()
    w_fi = dt("w_ffn_in", [DEPTH, D, 2 * DFF], F32, kind="ExternalInput").ap()
    w_fo = dt("w_ffn_out", [DEPTH, DFF, D], F32, kind="ExternalInput").ap()
    ln2g = dt("ln2_g", [DEPTH, D], F32, kind="ExternalInput").ap()
    ln2b = dt("ln2_b", [DEPTH, D], F32, kind="ExternalInput").ap()
    ident_in = dt("ident", [128, 128], F32, kind="ExternalInput").ap()
    mask_in = dt("maskbias", [128, 12, 512], F32, kind="ExternalInput").ap()
    ppool_in = dt("ppool", [128, 16, 128], F32, kind="ExternalInput").ap()
    out_d = dt("out", [TOWN, D], F32, kind="ExternalOutput").ap()
    if debug:
        dbg_d = dt("dbg", [REGIONS[0][1], D], F32, kind="ExternalOutput").ap()

    kw = dict(kind="ExternalOutput") if debug else {}
    QT = [dt(f"QT{g}", [4, 128, TL], BF16, **kw).ap() for g in range(3)]
    KT = [dt(f"KT{g}", [4, 128, PADS[g] + TL], BF16, **kw).ap() for g in range(3)]
    VS = [dt(f"VS{g}", [RATES[g], 64 + TL // RATES[g], 1024], BF16, **kw).ap() for g in range(3)]
    US = dt("US", [TL, 512], BF16, **kw).ap()
    HT = dt("HT", [8, 128, TL], BF16, **kw).ap()
    X1 = dt("X1", [TL, D], F32, **kw).ap()
    if debug:
        dbg_att = dt("dbg_att", [4, 128, TL], BF16, kind="ExternalOutput").ap()
        dbg_mrg = dt("dbg_mrg", [8, 128, TL], BF16, kind="ExternalOutput").ap()
        dbg_xmid = dt("dbg_xmid", [TL, D], F32, kind="ExternalOutput").ap()
        dbg_mod = dt("dbg_mod", [128, 32 + 2048], F32, kind="ExternalOutput").ap()

    stack = ExitStack()
    with stack:
        ARENA_W = 52224
        arena = stack.enter_context(nc.sbuf_tensor("arena", [128, ARENA_W], F32))
        psb = [stack.enter_context(nc.psum_tensor(f"psb{i}", [128, 512], F32)) for i in range(8)]

        class Alloc:
            def __init__(self, base, limit):
                self.o = base
                self.limit = limit

            def f32(self, n, shape3=None):
                a = arena[:, self.o:self.o + n]
                self.o += n
                assert self.o <= self.limit, (self.o, self.limit)
                if shape3:
                    a = a.rearrange("p (a b) -> p a b", b=shape3)
                return a

            def bf(self, n, shape3=None):
                w = (n + 1) // 2
                a = arena[:, self.o:self.o + w].bitcast(BF16)
                self.o += w
                assert self.o <= self.limit, (self.o, self.limit)
                if shape3:
                    a = a.rearrange("p (a b) -> p a b", b=shape3)
                return a

        pa = Alloc(0, 3500)
        ident = pa.bf(128)
        ppool = pa.bf(16 * 128, 128)
        cols = pa.f32(32)
        psc = pa.f32(4)
        ones_row = pa.f32(128)
        G1 = pa.f32(1024)
        G2 = pa.f32(1024)
        stats = pa.f32(128)
        XB = pa.o
        XL = ARENA_W

        psi = [0]

        def bank():
            i = psi[0] % 8
            psi[0] += 1
            return i

        def PSK(i):
            return ("ps", i)

        pending = []

        def act(out, in_, func, R, W, bias=0.0, scale=1.0):
            return P.op("scalar", lambda e: e.activation(out=out, in_=in_, func=func, bias=bias, scale=scale), R, W)

        def vts(out, in0, s1, s2, op0, op1, R, W):
            if s2 is None:
                return P.op("vector", lambda e: e.tensor_scalar(out=out, in0=in0, scalar1=s1, scalar2=None, op0=op0), R, W)
            return P.op("vector", lambda e: e.tensor_scalar(out=out, in0=in0, scalar1=s1, scalar2=s2, op0=op0, op1=op1), R, W)

        def vtt(out, in0, in1, op, R, W):
            return P.op("vector", lambda e: e.tensor_tensor(out=out, in0=in0, in1=in1, op=op), R, W)

        def vstt(out, in0, scalar, in1, op0, op1, R, W):
            return P.op("vector", lambda e: e.scalar_tensor_tensor(out=out, in0=in0, scalar=scalar, in1=in1, op0=op0, op1=op1), R, W)

        def vcopy(out, in_, R, W):
            return P.op("vector", lambda e: e.tensor_copy(out=out, in_=in_), R, W)

        def mm(out, lhsT, rhs, start, stop, R, W, sig):
            return P.pe(lambda e: e.matmul(out, lhsT=lhsT, rhs=rhs, start=start, stop=stop), R, W, sig)

        def ln_stats4(xaps, keys, tag):
            mv4 = stats[:, 48:56]
            ve = stats[:, 56:60]
            base = 64 + 16 * tag
            sd = stats[:, base:base + 4]
            rstd4 = stats[:, base + 4:base + 8]
            nb4 = stats[:, base + 8:base + 12]
            for s_ in range(4):
                st = stats[:, 12 * s_:12 * s_ + 12]
                xap = xaps[s_]
                P.op("vector", lambda e, st=st, xap=xap: e.bn_stats(out=st[:, 0:6], in_=xap[:, 0:512]), [keys[s_]], [("st", s_)])
                P.op("vector", lambda e, st=st, xap=xap: e.bn_stats(out=st[:, 6:12], in_=xap[:, 512:1024]), [keys[s_]], [("st2", s_)])
                P.op("vector", lambda e, st=st, s_=s_: e.bn_aggr(out=mv4[:, 2 * s_:2 * s_ + 2], in_=st), [("st", s_), ("st2", s_)], ["mv4"])
            vts(ve, mv4[:, 1:8:2], EPS, None, ALU.add, None, ["mv4"], ["ve"])
            act(sd, ve, AF.Sqrt, ["ve"], [("sd", tag)])
            P.op("vector", lambda e: e.reciprocal(out=rstd4, in_=sd), [("sd", tag)], [("rstd", tag)])
            vstt(nb4, mv4[:, 0:8:2], -1.0, rstd4, ALU.mult, ALU.mult, ["mv4", ("rstd", tag)], [("nb", tag)])
            return rstd4, nb4

        tmpc = arena[:, XB:XB + 2048 + 128]
        P.dma("sync", "c0", tmpc[:, 0:128], ident_in, W=["tmpc0"])
        vcopy(ident, tmpc[:, 0:128], ["tmpc0"], ["ident"])
        P.dma("sync", "c1", tmpc[:, 128:128 + 2048], ppool_in.rearrange("p a b -> p (a b)"), W=["tmpc1"])
        vcopy(ppool.rearrange("p a b -> p (a b)"), tmpc[:, 128:128 + 2048], ["tmpc1"], ["ppool"])
        P.op("vector", lambda e: e.memset(ones_row, 1.0), [], ["ones_row"])
        zt = arena[:, XB + 4096:XB + 4096 + 512].bitcast(BF16)
        P.op("vector", lambda e: e.memset(zt, 0.0), [], ["zt"])
        for g in range(3):
            for pr in range(4):
                P.dma("sync", "zp", KT[g][pr, :, 0:PADS[g]], zt[:, 0:PADS[g]], R=["zt"])
            for ph in range(RATES[g]):
                P.dma("sync", "zp", VS[g][ph, 0:64, :], zt[0:64, :], R=["zt"])
        P.barrier()

        def ck(name):
            if stop == name:
                P.barrier()
                raise _Stop()

        try:
          ck("consts")
          for l in range(DEPTH):
              Tkv, Tq = REGIONS[l]
              xsrc = x_in if l == 0 else X1
              xdst = X1 if l == 0 else out_d

              am = Alloc(XB, XL)
              modrow = am.f32(6144)
              wad = [am.f32(8 * 512, 512), am.f32(8 * 512, 512)]
              ccol = am.f32(8)
              scol = am.f32(8)
              badar = am.f32(6144)
              colps = am.f32(32)
              P.dma("sync", "m_c", ccol, c_col, W=["ccol"])
              P.dma("sync", "m_b", badar[0:1, :], b_ada[l:l + 1, :], W=["badar"])
              P.dma("sync", "m_psc", psc, psc_in[l], W=["psc"])
              act(scol, ccol, AF.Silu, ["ccol"], ["scol"])
              for blk in range(12):
                  wb = wad[blk % 2]
                  wk = ("wad", blk % 2)
                  P.dma("sync", f"m_w{blk % 2}", wb,
                        w_ada[l, :, blk * 512:(blk + 1) * 512].rearrange("(k p) n -> p k n", p=128), W=[wk])
                  b = bank()
                  for kc in range(8):
                      mm(psb[b][0:1, :], scol[:, kc:kc + 1], wb[:, kc, :], kc == 0, kc == 7,
                         [wk, "scol"], [PSK(b)], kc == 7)
                  vtt(modrow[0:1, blk * 512:(blk + 1) * 512], psb[b][0:1, :], badar[0:1, blk * 512:(blk + 1) * 512],
                      ALU.add, [PSK(b), "badar"], ["modrow"])
              for gi, (goff, Gt) in enumerate(((2048, G1), (5120, G2))):
                  vts(modrow[0:1, goff:goff + 1024], modrow[0:1, goff:goff + 1024], 1.0, None, ALU.add, ALU.bypass,
                      ["modrow"], ["modrow"])
                  for half in range(2):
                      b = bank()
                      mm(psb[b][:, :], ones_row[0:1, :], modrow[0:1, goff + half * 512:goff + (half + 1) * 512],
                         True, True, ["modrow", "ones_row"], [PSK(b)], True)
                      vcopy(Gt[:, half * 512:(half + 1) * 512], psb[b][:, :], [PSK(b)], [("G", gi)])
              b = bank()
              for qi, qoff in enumerate((0, 1024, 3072, 4096)):
                  for kc in range(8):
                      last = (qi == 3 and kc == 7)
                      mm(psb[b][:, qi * 8 + kc:qi * 8 + kc + 1], modrow[0:1, qoff + kc * 128:qoff + (kc + 1) * 128],
                         ones_row[0:1, 0:1], True, True, ["modrow", "ones_row"], [PSK(b)], last)
              vcopy(colps, psb[b][:, 0:32], [PSK(b)], ["colps"])
              vcopy(cols[:, 0:8], colps[:, 0:8], ["colps"], ["cols"])
              vts(cols[:, 8:16], colps[:, 8:16], 1.0, None, ALU.add, ALU.bypass, ["colps"], ["cols"])
              vcopy(cols[:, 16:24], colps[:, 16:24], ["colps"], ["cols"])
              vts(cols[:, 24:32], colps[:, 24:32], 1.0, None, ALU.add, ALU.bypass, ["colps"], ["cols"])
              if debug and l == 0:
                  P.dma("sync", "dbgmod", dbg_mod[:, 0:32], cols, R=["cols"])
                  P.dma("sync", "dbgmod", dbg_mod[:, 32:1056], G1, R=[("G", 0)])
                  P.dma("sync", "dbgmod", dbg_mod[:, 1056:2080], G2, R=[("G", 1)])
              P.barrier()
              ck(f"M{l}")

              aa = Alloc(XB, XL)
              WA = aa.bf(8 * 5120, 5120)
              xA = [aa.f32(4 * 1024, 1024) for _ in range(2)]
              xn = aa.bf(4 * 1024, 1024)
              hTb = [aa.bf(8 * 512, 512) for _ in range(2)]
              stg = [aa.bf(8 * 512, 512) for _ in range(2)]
              vst = [aa.bf(8 * 128, 128) for _ in range(2)]
              ust = [aa.bf(512) for _ in range(2)]
              for kc in range(8):
                  P.dma("gpsimd", "wa", WA[:, kc, :], w_in[l, kc * 128:(kc + 1) * 128, 0:5120], W=["WA"])
              for i in range(2):
                  P.op("vector", lambda e, i=i: e.memset(vst[i][:, :, 64:128], 1.0), [], [("vst", i)])
              evq = [0]
              vsi = [0]
              usi = [0]

              def evac(out, in_, R, W, scale=None):
                  evq[0] += 1
                  if evq[0] % 2 == 0:
                      if scale is None:
                          return vcopy(out, in_, R, W)
                      return vts(out, in_, scale, None, ALU.mult, ALU.bypass, R, W)
                  return act(out, in_, AF.Identity, R, W, scale=(1.0 if scale is None else scale))

              def a_ln(j):
                  t0 = 512 * j
                  hT = hTb[j % 2]
                  hb = j % 2
                  xa = xA[j % 2]
                  kx = ("xA", j % 2)
                  P.dma("sync", f"xA{j % 2}", xa, xsrc[t0:t0 + 512, :].rearrange("(s p) f -> p s f", p=128), W=[kx])
                  rstd4, nb4 = ln_stats4([xa[:, s, :] for s in range(4)], [kx] * 4, 0)
                  for s in range(4):
                      act(xn[:, s, :], xa[:, s, :], AF.Identity, [kx, ("rstd", 0), ("nb", 0)], [("xn", s)],
                          bias=nb4[:, s:s + 1], scale=rstd4[:, s:s + 1])
                  for kp in range(4):
                      b = bank()
                      pb = psb[b][:, :].bitcast(BF16)
                      for kk in range(2):
                          kc = 2 * kp + kk
                          for s in range(4):
                              P.pe(lambda e, o=pb[:, kk * 512 + s * 128:kk * 512 + (s + 1) * 128],
                                   i=xn[:, s, kc * 128:(kc + 1) * 128]: e.transpose(out=o, in_=i, identity=ident),
                                   [("xn", s), "ident"], [PSK(b)], sig=(kk == 1 and s == 3))
                      for kk in range(2):
                          kc = 2 * kp + kk
                          src = pb[:, kk * 512:(kk + 1) * 512]
                          if kp % 2 == 0:
                              vts(hT[:, kc, :], src, cols[:, 8 + kc:9 + kc], cols[:, kc:kc + 1], ALU.mult, ALU.add,
                                  [PSK(b), "cols"], [("hT", hb, kc)])
                          else:
                              act(hT[:, kc, :], src, AF.Identity, [PSK(b), "cols"], [("hT", hb, kc)],
                                  bias=cols[:, kc:kc + 1], scale=cols[:, 8 + kc:9 + kc])
                  hTk = [("hT", hb, kc) for kc in range(8)]
                  P.dma("sync", "hTst", HT[:, :, t0:t0 + 512].rearrange("k p t -> p k t"), hT, R=hTk)

              def a_qk(j):
                  t0 = 512 * j
                  hT = hTb[j % 2]
                  hb = j % 2
                  need_q = (t0 < Tq)
                  for g in range(3):
                      sg = stg[g % 2]
                      ks = ("stg", g % 2)
                      for which in ((0, 1) if need_q else (1,)):
                          for pr in range(4):
                              col0 = g * 1536 + which * 512 + pr * 128
                              b = bank()
                              for kc in range(8):
                                  mm(psb[b][:, :], WA[:, kc, col0:col0 + 128], hT[:, kc, :], kc == 0, kc == 7,
                                     ["WA", ("hT", hb, kc)], [PSK(b)], kc == 7)
                              evac(sg[:, which * 4 + pr, :], psb[b][:, :], [PSK(b)], [(ks, which * 4 + pr)],
                                   scale=(0.125 if which == 0 else None))
                      if need_q:
                          P.dma("sync", f"stg{g % 2}", QT[g][:, :, t0:t0 + 512].rearrange("a p t -> p a t"), sg[:, 0:4, :], R=[(ks, i_) for i_ in range(4)])
                      P.dma("sync", f"stg{g % 2}", KT[g][:, :, PADS[g] + t0:PADS[g] + t0 + 512].rearrange("a p t -> p a t"),
                            sg[:, 4:8, :], R=[(ks, i_) for i_ in range(4, 8)])

              def a_vu(j):
                  t0 = 512 * j
                  hT = hTb[j % 2]
                  hb = j % 2
                  for g in range(3):
                      r = RATES[g]
                      for blk in range(4):
                          if g == 0:
                              lcols = lambda kc, blk=blk: hT[:, kc, blk * 128:(blk + 1) * 128]
                          elif g == 1:
                              lcols = lambda kc, blk=blk: hT[:, kc, :].rearrange("p (n r) -> p r n", r=4)[:, blk, :]
                          else:
                              lcols = lambda kc, blk=blk: hT[:, kc, :].rearrange("p (n r) -> p r n", r=4)[:, blk, :]
                          b = bank()
                          c0 = g * 1536 + 1024
                          for kc in range(8):
                              mm(psb[b][:, :], lcols(kc), WA[:, kc, c0:c0 + 512], kc == 0, kc == 7,
                                 ["WA", ("hT", hb, kc)], [PSK(b)], kc == 7)
                          vi = vsi[0] % 2
                          vsi[0] += 1
                          evac(vst[vi][:, :, 0:64], psb[b][:, :].rearrange("p (a b) -> p a b", b=64), [PSK(b)], [("vst", vi)])
                          vflat = vst[vi].rearrange("p a b -> p (a b)")
                          if g == 0:
                              P.dma("sync", f"vst{vi}", VS[0][0, 64 + t0 + blk * 128:64 + t0 + (blk + 1) * 128, :], vflat, R=[("vst", vi)])
                          elif g == 1:
                              P.dma("sync", f"vst{vi}", VS[1][blk, 64 + t0 // 4:64 + t0 // 4 + 128, :], vflat, R=[("vst", vi)])
                          else:
                              for pl in range(4):
                                  P.dma("sync", f"vst{vi}", VS[2][blk + 4 * pl, 64 + t0 // 16:64 + t0 // 16 + 32, :],
                                        vflat[pl::4, :], R=[("vst", vi)])
                  for s in range(4):
                      if t0 + s * 128 >= Tq + 128:
                          continue
                      b = bank()
                      for kc in range(8):
                          mm(psb[b][:, :], hT[:, kc, s * 128:(s + 1) * 128], WA[:, kc, 4608:5120], kc == 0, kc == 7,
                             ["WA", ("hT", hb, kc)], [PSK(b)], kc == 7)
                      ui = usi[0] % 2
                      usi[0] += 1
                      evac(ust[ui], psb[b][:, :], [PSK(b)], [("ust", ui)])
                      P.dma("sync", f"ust{ui}", US[t0 + s * 128:t0 + (s + 1) * 128, :], ust[ui], R=[("ust", ui)])

              nta = Tkv // 512
              a_ln(0)
              for j in range(nta):
                  a_qk(j)
                  if j + 1 < nta:
                      a_ln(j + 1)
                  a_vu(j)
              P.barrier()
              ck(f"A{l}")

              ab = Alloc(XB, XL)
              attT = ab.bf(4 * 2048, 2048)
              BC0 = ab.o
              maskb = ab.bf(12 * 512, 512)
              QTs = [ab.bf(2048) for _ in range(3)]
              KTs = [ab.bf(2048 + 2 * PADS[g]) for g in range(3)]
              NVC = 17 + 20 + 32
              Vc = ab.bf(NVC * 256, 256)
              acc = ab.f32(2 * 2048, 2048)
              Rr = ab.f32(2 * 2048, 2048)
              tmpE = [ab.bf(1024) for _ in range(4)]
              Ep = [ab.bf(1024) for _ in range(4)]
              ac = Alloc(BC0, XL)
              lnr = [ac.f32(1024) for _ in range(4)]
              hTc = ac.bf(8 * 512, 512)
              uT = ac.bf(6 * 512, 512)
              pooledT = ac.bf(4 * 512, 512)
              pmT = ac.bf(4 * 512, 512)
              Wmix = ac.bf(4 * 128, 128)
              sg_ = [ac.f32(512), ac.f32(512)]
              t12 = [ac.f32(512), ac.f32(512)]
              mergedT = ac.bf(8 * 512, 512)
              xmid2 = [ac.f32(4 * 1024, 1024) for _ in range(2)]
              xs = ac.f32(1024)
              xnf = ac.f32(1024)
              xn2 = ac.bf(4 * 1024, 1024)
              h2T = ac.bf(8 * 512, 512)
              gT = ac.bf(22 * 512, 512)
              sa = [ac.f32(512), ac.f32(512)]
              tg = ac.f32(512)
              ring = [ac.bf(8 * 512, 512) for _ in range(4)]
              ringi = [0]

              def wload(src2d, nk, ncol):
                  i = ringi[0] % 4
                  ringi[0] += 1
                  slot = ring[i]
                  P.dma("gpsimd", f"ring{i}", slot[:, 0:nk, 0:ncol], src2d.rearrange("(k p) n -> p k n", p=128),
                        W=[("ring", i)])
                  return slot, ("ring", i)

              nsb = (Tq + 2047) // 2048
              for M in range(nsb):
                  sb0 = 2048 * M
                  nq = min(2048, Tq - sb0)
                  if M > 0:
                      P.barrier()
                  P.dma("gpsimd", "maskb", maskb, mask_in, W=["maskb"])
                  for pr in range(4):
                      for g in range(3):
                          P.dma("sync", f"qts{g}", QTs[g][:, 0:nq], QT[g][pr, :, sb0:sb0 + nq], W=[("QTs", g)])
                          P.dma("sync", f"kts{g}", KTs[g][:, 0:nq + 2 * PADS[g]], KT[g][pr, :, sb0:sb0 + nq + 2 * PADS[g]],
                                W=[("KTs", g)])
                      vidx = {}
                      ci = 0
                      for g in range(3):
                          r = RATES[g]
                          npos = nq // r
                          nqb = min(128, npos)
                          nQb = npos // nqb
                          for ph in range(r):
                              for m in range(nQb + 1):
                                  nk = 128 if m < nQb else nqb
                                  row0 = sb0 // r + 128 * m
                                  P.dma("sync", "vc", Vc[0:nk, ci, :], VS[g][ph, row0:row0 + nk, pr * 256:(pr + 1) * 256],
                                        W=["Vc"])
                                  vidx[(g, ph, m)] = ci
                                  ci += 1
                      ck("Bloads")
                      units = []
                      for g in range(3):
                          r = RATES[g]
                          npos = nq // r
                          nqb = min(128, npos)
                          nQb = npos // nqb
                          for ph in range(r):
                              for m in range(nQb):
                                  units.append((g, r, nqb, ph, m))
                      LOOK = 2
                      assert len(units) % 2 == 0
                      upairs = [(units[2 * i], units[2 * i + 1]) for i in range(len(units) // 2)]

                      def emit_S(idx):
                          bS = [(2 * idx) % 6, (2 * idx + 1) % 6]
                          ui = idx % 4
                          g = upairs[idx][0][0]
                          kq = [("QTs", g), ("KTs", g)]
                          for sl in range(2):
                              lo, hi = 64 * sl, 64 * sl + 64
                              ps = psb[bS[sl]]
                              for j in range(2):
                                  g_, r, nqb, ph, m = upairs[idx][j]
                                  qs = QTs[g][:, r * 128 * m + ph: r * 128 * m + ph + r * (nqb - 1) + 1: r]
                                  kA = KTs[g][:, r * 128 * m + ph: r * 128 * m + ph + r * 127 + 1: r]
                                  kB = KTs[g][:, r * 128 * (m + 1) + ph: r * 128 * (m + 1) + ph + r * (nqb - 1) + 1: r]
                                  c0 = 256 * j
                                  mm(ps[:, c0:c0 + nqb], kA[lo:hi, :], qs[lo:hi, :], True, True, kq, [PSK(bS[sl])], False)
                                  mm(ps[0:nqb, c0 + 128:c0 + 128 + nqb], kB[lo:hi, :], qs[lo:hi, :], True, True,
                                     kq, [PSK(bS[sl])], j == 1)
                              act(tmpE[ui].rearrange("p (j s c) -> p j s c", j=2, s=2)[:, :, sl, :],
                                  ps[:, :].rearrange("p (j c) -> p j c", j=2), AF.Exp,
                                  [PSK(bS[sl])], [("tmpE", ui, sl)])
                          for j in range(2):
                              vtt(Ep[ui][:, 512 * j:512 * (j + 1)], tmpE[ui][:, 512 * j:512 * (j + 1)], maskb[:, g * 4 + pr, :], ALU.mult,
                                  [("tmpE", ui, 0), ("tmpE", ui, 1), "maskb"], [("Ep", ui, j)])

                      def emit_PV(idx):
                          ui = idx % 4
                          b2 = 6 + idx % 2
                          ps2 = psb[b2]
                          for j in range(2):
                              g, r, nqb, ph, m = upairs[idx][j]
                              cA = vidx[(g, ph, m)]
                              cB = vidx[(g, ph, m + 1)]
                              e0 = 512 * j
                              for sl in range(2):
                                  o = ps2[:, j * 256 + sl * 128:j * 256 + sl * 128 + nqb]
                                  mm(o, Vc[:, cA, sl * 128:(sl + 1) * 128],
                                     Ep[ui][:, e0 + sl * 256:e0 + sl * 256 + nqb], True, False, ["Vc", ("Ep", ui, j)], [PSK(b2)], False)
                                  mm(o, Vc[0:nqb, cB, sl * 128:(sl + 1) * 128],
                                     Ep[ui][0:nqb, e0 + sl * 256 + 128:e0 + sl * 256 + 128 + nqb], False, True,
                                     ["Vc", ("Ep", ui, j)], [PSK(b2)], (j == 1 and sl == 1))
                          for j in range(2):
                              g, r, nqb, ph, m = upairs[idx][j]
                              acv = acc[:, :, r * 128 * m + ph: r * 128 * m + ph + r * (nqb - 1) + 1: r]
                              pv = ps2[:, j * 256:(j + 1) * 256].rearrange("p (a b) -> p a b", b=128)[:, :, 0:nqb]
                              if g == 0:
                                  vcopy(acv, pv, [PSK(b2)], ["acc"])
                              else:
                                  vtt(acv, acv, pv, ALU.add, [PSK(b2), "acc"], ["acc"])

                      for idx in range(len(upairs) + LOOK):
                          if idx < len(upairs):
                              emit_S(idx)
                          if idx - LOOK >= 0:
                              emit_PV(idx - LOOK)
                      act(Rr[0:64, :, 0:nq], acc[64:128, :, 0:nq], AF.Ln, ["acc"], ["Rr"])
                      act(Rr[0:64, :, 0:nq], Rr[0:64, :, 0:nq], AF.Exp, ["Rr"], ["Rr"], scale=-1.0)
                      vtt(attT[0:64, pr, 0:nq], acc[0:64, 0, 0:nq], Rr[0:64, 0, 0:nq], ALU.mult, ["acc", "Rr"], ["attT"])
                      vtt(attT[64:128, pr, 0:nq], acc[0:64, 1, 0:nq], Rr[0:64, 1, 0:nq], ALU.mult, ["acc", "Rr"], ["attT"])
                      if debug and l == 0:
                          P.dma("sync", "dbgatt", dbg_att[pr, :, sb0:sb0 + nq], attT[:, pr, 0:nq], R=["attT"])
                      ck("Bn")

                  P.barrier()
                  ck(f"B{l}_{M}")
                  P.dma("gpsimd", "wmix", Wmix, w_mix[l].rearrange("g c d -> c g d"), W=["Wmix"])
                  for i_, src_ in enumerate((ln1g, ln1b, ln2g, ln2b)):
                      P.dma("sync", f"lnr{i_}", lnr[i_], src_[l:l + 1, :].partition_broadcast(128).rearrange("p a f -> p (a f)"),
                            W=[("lnr", i_)])

                  def drain(n):
                      while n > 0 and pending:
                          pending.pop(0)()
                          n -= 1

                  def c_front(tt):
                      t0 = sb0 + 512 * tt
                      tl0 = 512 * tt
                      tb = tt % 2
                      xm = xmid2[tb]
                      P.dma("sync", "hTc", hTc, HT[:, :, t0:t0 + 512].rearrange("k p t -> p k t"), W=["hTc"])
                      if t0 == 0:
                          P.dma("sync", "uT", uT[:, 1:6, :], US[0:640, :].rearrange("(s p) f -> p s f", p=128), W=["uT"])
                      else:
                          P.dma("sync", "uT", uT, US[t0 - 128:t0 + 640, :].rearrange("(s p) f -> p s f", p=128), W=["uT"])
                      for g in range(4):
                          b = bank()
                          for bl in range(4):
                              first = (t0 == 0 and bl == 0)
                              kind = 3 if first else 0
                              o = psb[b][:, bl * 128:(bl + 1) * 128]
                              mm(o, uT[:, bl + 1, g * 128:(g + 1) * 128], ppool[:, g * 4 + kind, :], True, False,
                                 ["uT", "ppool"], [PSK(b)], False)
                              if not first:
                                  mm(o, uT[:, bl, g * 128:(g + 1) * 128], ppool[:, g * 4 + 1, :], False, False,
                                     ["uT", "ppool"], [PSK(b)], False)
                              mm(o, uT[:, bl + 2, g * 128:(g + 1) * 128], ppool[:, g * 4 + 2, :], False, True,
                                 ["uT", "ppool"], [PSK(b)], bl == 3)
                          act(pooledT[:, g, :], psb[b][:, :], AF.Identity, [PSK(b)], [("pooledT", g)])
                          b2 = bank()
                          mm(psb[b2][:, :], Wmix[:, g, :], pooledT[:, g, :], True, True, ["Wmix", ("pooledT", g)], [PSK(b2)], True)
                          vts(pmT[:, g, :], psb[b2][:, :], psc[:, g:g + 1], None, ALU.mult, ALU.bypass, [PSK(b2), "psc"], [("pmT", g)])
                      pmk = [("pmT", g) for g in range(4)]
                      for jb in range(2):
                          Wga, kga = wload(w_in[l, :, 5120 + 512 * jb:5120 + 512 * (jb + 1)], 8, 512)
                          Wgp, kgp = wload(w_in[l, :, 6144 + 512 * jb:6144 + 512 * (jb + 1)], 8, 512)
                          Wao, kao = wload(w_ao[l, :, 512 * jb:512 * (jb + 1)], 4, 512)
                          Wpo, kpo = wload(w_po[l, :, 512 * jb:512 * (jb + 1)], 4, 512)
                          for cc in range(4):
                              c = 4 * jb + cc
                              csl = slice(cc * 128, (cc + 1) * 128)
                              bga, bgp, bya, byp = bank(), bank(), bank(), bank()
                              for kc in range(8):
                                  mm(psb[bga][:, :], Wga[:, kc, csl], hTc[:, kc, :], kc == 0, kc == 7, [kga, "hTc"], [PSK(bga)], kc == 7)
                              for kc in range(8):
                                  mm(psb[bgp][:, :], Wgp[:, kc, csl], hTc[:, kc, :], kc == 0, kc == 7, [kgp, "hTc"], [PSK(bgp)], kc == 7)
                              for kc in range(4):
                                  mm(psb[bya][:, :], Wao[:, kc, csl], attT[:, kc, tl0:tl0 + 512], kc == 0, kc == 3, [kao, "attT"], [PSK(bya)], kc == 3)
                              for kc in range(4):
                                  mm(psb[byp][:, :], Wpo[:, kc, csl], pmT[:, kc, :], kc == 0, kc == 3, [kpo] + pmk, [PSK(byp)], kc == 3)
                              act(sg_[0], psb[bga][:, :], AF.Sigmoid, [PSK(bga)], [("sg", 0)])
                              act(sg_[1], psb[bgp][:, :], AF.Sigmoid, [PSK(bgp)], [("sg", 1)])
                              vtt(t12[0], sg_[0], psb[bya][:, :], ALU.mult, [("sg", 0), PSK(bya)], [("t12", 0)])
                              vtt(t12[1], sg_[1], psb[byp][:, :], ALU.mult, [("sg", 1), PSK(byp)], [("t12", 1)])
                              vtt(mergedT[:, c, :], t12[0], t12[1], ALU.add, [("t12", 0), ("t12", 1)], [("mergedT", c)])
                              drain(1)
                      mk = [("mergedT", c) for c in range(8)]
                      if debug and l == 0:
                          P.dma("sync", "dbgmrg", dbg_mrg[:, :, t0:t0 + 512].rearrange("k p t -> p k t"), mergedT, R=mk)
                      for half in range(2):
                          Wo, kwo = wload(w_o[l, :, 512 * half:512 * (half + 1)], 8, 512)
                          for s in range(4):
                              b = bank()
                              for kc in range(8):
                                  mm(psb[b][:, :], mergedT[:, kc, s * 128:(s + 1) * 128], Wo[:, kc, :], kc == 0, kc == 7,
                                     [kwo] + mk, [PSK(b)], kc == 7)
                              vtt(xm[:, s, half * 512:(half + 1) * 512], psb[b][:, :], G1[:, half * 512:(half + 1) * 512],
                                  ALU.mult, [PSK(b), ("G", 0)], [("xmid", tb, s)])

                  def c_ln1(tt):
                      t0 = sb0 + 512 * tt
                      tb = tt % 2
                      xm = xmid2[tb]
                      xmk = [("xmid", tb, s) for s in range(4)]
                      st_ = {}
                      pieces = []

                      def p_res(s):
                          P.dma("sync", "xs", xs, xsrc[t0 + s * 128:t0 + (s + 1) * 128, :], W=["xs"])
                          vstt(xm[:, s, :], xs, ALPHA, xm[:, s, :], ALU.mult, ALU.add, ["xs", ("xmid", tb, s)], [("xmid", tb, s)])

                      def p_stats1():
                          st_["a"] = ln_stats4([xm[:, s, :] for s in range(4)], xmk, 1)

                      def p_aff(s):
                          rstd4, nb4 = st_["a"]
                          act(xnf, xm[:, s, :], AF.Identity, [("xmid", tb, s), ("rstd", 1), ("nb", 1)], ["xnf"],
                              bias=nb4[:, s:s + 1], scale=rstd4[:, s:s + 1])
                          vtt(xm[:, s, :], xnf, lnr[0], ALU.mult, ["xnf", ("lnr", 0)], [("xmid", tb, s)])
                          vtt(xm[:, s, :], xm[:, s, :], lnr[1], ALU.add, [("xmid", tb, s), ("lnr", 1)], [("xmid", tb, s)])

                      def p_stats2():
                          if debug and l == 0:
                              P.dma("sync", "dbgxmid", dbg_xmid[t0:t0 + 512, :].rearrange("(s p) f -> p s f", p=128), xm, R=xmk)
                          st_["b"] = ln_stats4([xm[:, s, :] for s in range(4)], xmk, 2)

                      def p_xn2(s):
                          rstd4, nb4 = st_["b"]
                          act(xn2[:, s, :], xm[:, s, :], AF.Identity, [("xmid", tb, s), ("rstd", 2), ("nb", 2)], [("xn2", s)],
                              bias=nb4[:, s:s + 1], scale=rstd4[:, s:s + 1])
                      for s in range(4):
                          pieces.append(lambda s=s: p_res(s))
                      pieces.append(p_stats1)
                      for s in range(4):
                          pieces.append(lambda s=s: p_aff(s))
                      pieces.append(p_stats2)
                      for s in range(4):
                          pieces.append(lambda s=s: p_xn2(s))
                      return pieces

                  def c_back(tt):
                      tb = tt % 2
                      xm = xmid2[tb]
                      for kp in range(4):
                          b = bank()
                          pb = psb[b][:, :].bitcast(BF16)
                          for kk in range(2):
                              kc = 2 * kp + kk
                              for s in range(4):
                                  P.pe(lambda e, o=pb[:, kk * 512 + s * 128:kk * 512 + (s + 1) * 128],
                                       i=xn2[:, s, kc * 128:(kc + 1) * 128]: e.transpose(out=o, in_=i, identity=ident),
                                       [("xn2", s), "ident"], [PSK(b)], sig=(kk == 1 and s == 3))
                          for kk in range(2):
                              kc = 2 * kp + kk
                              src = pb[:, kk * 512:(kk + 1) * 512]
                              if kp % 2 == 0:
                                  vts(h2T[:, kc, :], src, cols[:, 24 + kc:25 + kc], cols[:, 16 + kc:17 + kc], ALU.mult, ALU.add,
                                      [PSK(b), "cols"], [("h2T", kc)])
                              else:
                                  act(h2T[:, kc, :], src, AF.Identity, [PSK(b), "cols"], [("h2T", kc)],
                                      bias=cols[:, 16 + kc:17 + kc], scale=cols[:, 24 + kc:25 + kc])
                      h2k = [("h2T", kc) for kc in range(8)]
                      for jb in range(6):
                          bw = 512 if jb < 5 else 256
                          Wa, kwa = wload(w_fi[l, :, 512 * jb:512 * jb + bw], 8, bw)
                          Wb, kwb = wload(w_fi[l, :, DFF + 512 * jb:DFF + 512 * jb + bw], 8, bw)
                          for cc in range(bw // 128):
                              c = 4 * jb + cc
                              csl = slice(cc * 128, (cc + 1) * 128)
                              ba, bb_ = bank(), bank()
                              for kc in range(8):
                                  mm(psb[ba][:, :], Wa[:, kc, csl], h2T[:, kc, :], kc == 0, kc == 7, [kwa] + h2k, [PSK(ba)], kc == 7)
                              for kc in range(8):
                                  mm(psb[bb_][:, :], Wb[:, kc, csl], h2T[:, kc, :], kc == 0, kc == 7, [kwb] + h2k, [PSK(bb_)], kc == 7)
                              si = c % 2
                              act(sa[si], psb[ba][:, :], AF.Silu, [PSK(ba)], [("sa", si)])
                              vtt(gT[:, c, :], sa[si], psb[bb_][:, :], ALU.mult, [("sa", si), PSK(bb_)], [("gT", c)])
                              drain(1)
                      gk = [("gT", c) for c in range(22)]
                      for half in range(2):
                          bF = [bank() for _ in range(4)]
                          for kg, (k0, nk) in enumerate(((0, 8), (8, 8), (16, 6))):
                              Wf, kwf = wload(w_fo[l, k0 * 128:(k0 + nk) * 128, 512 * half:512 * (half + 1)], nk, 512)
                              for s in range(4):
                                  for kk in range(nk):
                                      kc = k0 + kk
                                      mm(psb[bF[s]][:, :], gT[:, kc, s * 128:(s + 1) * 128], Wf[:, kk, :], kc == 0, kc == 21,
                                         [kwf] + gk, [PSK(bF[s])], kk == nk - 1)
                          for s in range(4):
                              hs = slice(half * 512, (half + 1) * 512)
                              vtt(tg, psb[bF[s]][:, :], G2[:, hs], ALU.mult, [PSK(bF[s]), ("G", 1)], ["tg"])
                              vstt(xm[:, s, hs], xm[:, s, hs], ALPHA, tg, ALU.mult, ALU.add, ["tg", ("xmid", tb, s)], [("xmid", tb, s)])

                  def c_ln2(tt):
                      t0 = sb0 + 512 * tt
                      tb = tt % 2
                      xm = xmid2[tb]
                      xmk = [("xmid", tb, s) for s in range(4)]
                      st_ = {}

                      def p_stats():
                          st_["a"] = ln_stats4([xm[:, s, :] for s in range(4)], xmk, 3)

                      def p_fin(s):
                          rstd4, nb4 = st_["a"]
                          act(xnf, xm[:, s, :], AF.Identity, [("xmid", tb, s), ("rstd", 3), ("nb", 3)], ["xnf"],
                              bias=nb4[:, s:s + 1], scale=rstd4[:, s:s + 1])
                          vtt(xm[:, s, :], xnf, lnr[2], ALU.mult, ["xnf", ("lnr", 2)], [("xmid", tb, s)])
                          vtt(xm[:, s, :], xm[:, s, :], lnr[3], ALU.add, [("xmid", tb, s), ("lnr", 3)], [("xmid", tb, s)])
                          r0 = t0 + s * 128
                          P.dma("sync", f"xo{s % 2}", xdst[r0:r0 + 128, :], xm[:, s, :], R=[("xmid", tb, s)])
                          if debug and l == 0:
                              P.dma("sync", f"xo{s % 2}", dbg_d[r0:r0 + 128, :], xm[:, s, :], R=[("xmid", tb, s)])
                      return [p_stats] + [(lambda s=s: p_fin(s)) for s in range(4)]

                  ntile = nq // 512
                  pending.clear()
                  c_front(0)
                  pending.extend(c_ln1(0))
                  if ntile == 1:
                      drain(99)
                  for tt in range(ntile):
                      if tt + 1 < ntile:
                          c_front(tt + 1)
                      drain(99)
                      if tt + 1 < ntile:
                          pending.extend(c_ln1(tt + 1))
                      c_back(tt)
                      drain(99)
                      pending.extend(c_ln2(tt))
                  drain(99)
              P.barrier()
        except _Stop:
            pass

        P.emit(nc, stack)
    return nc, P


_CACHE = {}


def _consts(flip):
    n = 24
    slopes = np.exp2(-8.0 * np.arange(1, n + 1, dtype=np.float64) / n).reshape(3, 8)
    kk = np.arange(128)[:, None]
    qq = np.arange(128)[None, :]
    mask = np.zeros((128, 12, 512), np.float32)
    for g in range(3):
        r = RATES[g]
        for pr in range(4):
            for sl in range(2):
                s = slopes[g, 2 * pr + sl]
                for ab, rel in enumerate((kk - 64 - qq, kk + 64 - qq)):
                    m = np.where(np.abs(rel) <= 64, -s * r * np.abs(rel), NEG)
                    mask[:, g * 4 + pr, sl * 256 + ab * 128:sl * 256 + (ab + 1) * 128] = m
    pp = np.zeros((128, 16, 128), np.float32)
    for g, w in enumerate((2, 4, 8, 16)):
        h = w // 2
        lo_off, hi_off = (-h, h - 1) if not flip else (-h + 1, h)
        for t in range(128):
            for kind in range(4):
                for sabs in range(t + lo_off, t + hi_off + 1):
                    if kind in (0, 3):
                        s_loc = sabs
                    elif kind == 1:
                        s_loc = sabs + 128
                    else:
                        s_loc = sabs - 128
                    if 0 <= s_loc < 128:
                        if kind == 3:
                            lo = max(t + lo_off, 0)
                            cnt = t + hi_off - lo + 1
                        else:
                            cnt = w
                        pp[s_loc, g * 4 + kind, t] += 1.0 / cnt
                if kind in (0, 3):
                    pp[t, g * 4 + kind, t] -= 1.0
    ident = np.eye(128, dtype=np.float32)
    return mask, pp, ident


def kernel(x, c, w_ada, b_ada, w_in, w_pool_mix, pool_scale, w_att_out, w_pool_out, w_o,
           ln1_g, ln1_b, w_ffn_in, w_ffn_out, ln2_g, ln2_b, _debug=False):
    f = lambda a: np.ascontiguousarray(np.asarray(a, dtype=np.float32))
    x = f(x)
    c = f(c)
    shared = dict(w_ada=f(w_ada), b_ada=f(b_ada), w_in=f(w_in), w_pool_mix=f(w_pool_mix),
                  w_att_out=f(w_att_out), w_pool_out=f(w_pool_out), w_o=f(w_o), ln1_g=f(ln1_g), ln1_b=f(ln1_b),
                  w_ffn_in=f(w_ffn_in), w_ffn_out=f(w_ffn_out), ln2_g=f(ln2_g), ln2_b=f(ln2_b))
    ps = f(pool_scale)
    shared["pool_scale_col"] = np.ascontiguousarray(ps.reshape(DEPTH, 4, 128).transpose(0, 2, 1))
    key = "dbg" if _debug else "main"
    if key not in _CACHE:
        _CACHE[key] = build_program(debug=_debug)[0]
    nc = _CACHE[key]
    in_maps = []
    for core in range(8):
        b, half = core // 2, core % 2
        if half == 0:
            xl = x[b, 0:TL]
        else:
            xl = x[b, SEQ - TL:SEQ][::-1]
        mask, pp, ident = _consts(flip=(half == 1))
        m = dict(shared)
        m["x"] = np.ascontiguousarray(xl)
        m["c_col"] = np.ascontiguousarray(c[b].reshape(8, 128).T)
        m["maskbias"] = np.where(mask <= NEG / 2, 0.0, np.exp(mask.astype(np.float64))).astype(np.float32)
        m["ppool"] = pp
        m["ident"] = ident
        in_maps.append(m)
    if isinstance(_debug, int) and not isinstance(_debug, bool):
        res = run_bass_kernel_spmd(nc, [in_maps[_debug]], core_ids=[0])
        return res.results[0]
    res = run_bass_kernel_spmd(nc, in_maps, core_ids=list(range(8)))
    out = np.empty((BATCH, SEQ, D), np.float32)
    for core in range(8):
        b, half = core // 2, core % 2
        o = np.asarray(res.results[core]["out"], dtype=np.float32)
        if half == 0:
            out[b, 0:TOWN] = o
        else:
            out[b, TOWN:SEQ] = o[::-1]
    if _debug:
        return out, [np.asarray(r["dbg"]) for r in res.results]
    return out
```

```python
import numpy as np
import ml_dtypes
from contextlib import ExitStack
import concourse.bass as bass
import concourse.mybir as mybir
from concourse.bass_utils import run_bass_kernel_spmd

F32 = mybir.dt.float32
BF16 = mybir.dt.bfloat16
AF = mybir.ActivationFunctionType
ALU = mybir.AluOpType

D = 1024
SEQ = 8192
BATCH = 4
DEPTH = 2
TL = 6144
TOWN = 4096
RATES = (1, 4, 16)
PADS = (64, 256, 1024)
DFF = 2816
ALPHA = (2 * DEPTH) ** 0.25
EPS = 1e-5
REGIONS = ((6144, 5120), (5120, 4096))
NEG = -30000.0


class Prog:
    ENG = ("sync", "scalar", "gpsimd", "vector", "tensor")

    def __init__(self):
        self.ops = {e: [] for e in self.ENG}
        self.semcnt = {}
        self.lastw = {}
        self.readers = {}
        self.waited = {e: {} for e in self.ENG}
        self.nops = 0

    def _sem(self, name):
        if name not in self.semcnt:
            self.semcnt[name] = 0
        return name

    def _collect(self, eng, R, W, skip_same_chan=None):
        waits = {}

        def add(t):
            if t is None:
                return
            s, v = t
            if eng == "tensor" and s == "E_tensor":
                return
            if skip_same_chan is not None and s == skip_same_chan:
                return
            if waits.get(s, 0) < v:
                waits[s] = v
        for k in R:
            add(self.lastw.get(k))
        for k in W:
            add(self.lastw.get(k))
            for s, v in self.readers.get(k, {}).items():
                add((s, v))
        out = []
        wd = self.waited[eng]
        for s, v in waits.items():
            if wd.get(s, 0) < v:
                wd[s] = v
                out.append((s, v))
        return out

    def _commit(self, ticket, R, W):
        for k in R:
            d = self.readers.setdefault(k, {})
            if d.get(ticket[0], 0) < ticket[1]:
                d[ticket[0]] = ticket[1]
        for k in W:
            self.lastw[k] = ticket
            self.readers[k] = {}

    def op(self, eng, fn, R=(), W=()):
        sem = self._sem("E_" + eng)
        waits = self._collect(eng, R, W)
        self.semcnt[sem] += 1
        ticket = (sem, self.semcnt[sem])
        self._commit(ticket, R, W)
        self.ops[eng].append((waits, fn, (sem, 1)))
        self.nops += 1
        return ticket

    def pe(self, fn, R=(), W=(), sig=True):
        sem = self._sem("E_tensor")
        waits = self._collect("tensor", R, W)
        ticket = (sem, self.semcnt[sem] + 1)
        if sig:
            self.semcnt[sem] += 1
        self._commit(ticket, R, W)
        self.ops["tensor"].append((waits, fn, (sem, 1) if sig else None))
        self.nops += 1
        return ticket

    def dma(self, eng, chan, out, in_, R=(), W=()):
        sem = self._sem("D_" + chan)
        waits = self._collect(eng, R, W, skip_same_chan=sem)
        self.semcnt[sem] += 16
        ticket = (sem, self.semcnt[sem])
        self._commit(ticket, R, W)
        self.ops[eng].append((waits, lambda e, o=out, i=in_: e.dma_start(out=o, in_=i), (sem, 16)))
        self.nops += 1
        return ticket

    def barrier(self):
        allw = [(s, v) for s, v in self.semcnt.items() if v > 0]
        for e in self.ENG:
            ws = []
            for s, v in allw:
                if e == "tensor" and s == "E_tensor":
                    continue
                if self.waited[e].get(s, 0) < v:
                    self.waited[e][s] = v
                    ws.append((s, v))
            if ws:
                self.ops[e].append((ws, None, None))
        self.lastw = {}
        self.readers = {}

    def emit(self, nc, stack):
        sems = {name: stack.enter_context(nc.semaphore(name)) for name in self.semcnt}
        block = stack.enter_context(nc.Block())

        def run(eng_name):
            def body(e):
                for waits, fn, inc in self.ops[eng_name]:
                    for s, v in waits:
                        e.wait_ge(sems[s], v)
                    if fn is not None:
                        ins = fn(e)
                        if inc is not None:
                            ins.then_inc(sems[inc[0]], inc[1])
            return body
        block.sync(run("sync"))
        block.scalar(run("scalar"))
        block.gpsimd(run("gpsimd"))
        block.vector(run("vector"))
        block.tensor(run("tensor"))


class _Stop(Exception):
    pass


def build_program(debug=False, stop=None):
    nc = bass.Bass("TRN2", target_bir_lowering=False)
    P = Prog()
    dt = nc.dram_tensor

    x_in = dt("x", [TL, D], F32, kind="ExternalInput").ap()
    c_col = dt("c_col", [128, 8], F32, kind="ExternalInput").ap()
    w_ada = dt("w_ada", [DEPTH, D, 6 * D], F32, kind="ExternalInput").ap()
    b_ada = dt("b_ada", [DEPTH, 6 * D], F32, kind="ExternalInput").ap()
    w_in = dt("w_in", [DEPTH, D, 7168], F32, kind="ExternalInput").ap()
    w_mix = dt("w_pool_mix", [DEPTH, 4, 128, 128], F32, kind="ExternalInput").ap()
    psc_in = dt("pool_scale_col", [DEPTH, 128, 4], F32, kind="ExternalInput").ap()
    w_ao = dt("w_att_out", [DEPTH, 512, D], F32, kind="ExternalInput").ap()
    w_po = dt("w_pool_out", [DEPTH, 512, D], F32, kind="ExternalInput").ap()
    w_o = dt("w_o", [DEPTH, D, D], F32, kind="ExternalInput").ap()
    ln1g = dt("ln1_g", [DEPTH, D], F32, kind="ExternalInput").ap()
    ln1b = dt("ln1_b", [DEPTH, D], F32, kind="ExternalInput").ap()
    w_fi = dt("w_ffn_in", [DEPTH, D, 2 * DFF], F32, kind="ExternalInput").ap()
    w_fo = dt("w_ffn_out", [DEPTH, DFF, D], F32, kind="ExternalInput").ap()
    ln2g = dt("ln2_g", [DEPTH, D], F32, kind="ExternalInput").ap()
    ln2b = dt("ln2_b", [DEPTH, D], F32, kind="ExternalInput").ap()
    ident_in = dt("ident", [128, 128], F32, kind="ExternalInput").ap()
    mask_in = dt("maskbias", [128, 12, 512], F32, kind="ExternalInput").ap()
    ppool_in = dt("ppool", [128, 16, 128], F32, kind="ExternalInput").ap()
    out_d = dt("out", [TOWN, D], F32, kind="ExternalOutput").ap()
    if debug:
        dbg_d = dt("dbg", [REGIONS[0][1], D], F32, kind="ExternalOutput").ap()

    kw = dict(kind="ExternalOutput") if debug else {}
    QT = [dt(f"QT{g}", [4, 128, TL], BF16, **kw).ap() for g in range(3)]
    KT = [dt(f"KT{g}", [4, 128, PADS[g] + TL], BF16, **kw).ap() for g in range(3)]
    VS = [dt(f"VS{g}", [RATES[g], 64 + TL // RATES[g], 1024], BF16, **kw).ap() for g in range(3)]
    US = dt("US", [TL, 512], BF16, **kw).ap()
    HT = dt("HT", [8, 128, TL], BF16, **kw).ap()
    X1 = dt("X1", [TL, D], F32, **kw).ap()
    if debug:
        dbg_att = dt("dbg_att", [4, 128, TL], BF16, kind="ExternalOutput").ap()
        dbg_mrg = dt("dbg_mrg", [8, 128, TL], BF16, kind="ExternalOutput").ap()
        dbg_xmid = dt("dbg_xmid", [TL, D], F32, kind="ExternalOutput").ap()
        dbg_mod = dt("dbg_mod", [128, 32 + 2048], F32, kind="ExternalOutput").ap()

    stack = ExitStack()
    with stack:
        ARENA_W = 52224
        arena = stack.enter_context(nc.sbuf_tensor("arena", [128, ARENA_W], F32))
        psb = [stack.enter_context(nc.psum_tensor(f"psb{i}", [128, 512], F32)) for i in range(8)]

        class Alloc:
            def __init__(self, base, limit):
                self.o = base
                self.limit = limit

            def f32(self, n, shape3=None):
                a = arena[:, self.o:self.o + n]
                self.o += n
                assert self.o <= self.limit, (self.o, self.limit)
                if shape3:
                    a = a.rearrange("p (a b) -> p a b", b=shape3)
                return a

            def bf(self, n, shape3=None):
                w = (n + 1) // 2
                a = arena[:, self.o:self.o + w].bitcast(BF16)
                self.o += w
                assert self.o <= self.limit, (self.o, self.limit)
                if shape3:
                    a = a.rearrange("p (a b) -> p a b", b=shape3)
                return a

        pa = Alloc(0, 3500)
        ident = pa.bf(128)
        ppool = pa.bf(16 * 128, 128)
        cols = pa.f32(32)
        psc = pa.f32(4)
        ones_row = pa.f32(128)
        G1 = pa.f32(1024)
        G2 = pa.f32(1024)
        stats = pa.f32(128)
        XB = pa.o
        XL = ARENA_W

        psi = [0]

        def bank():
            i = psi[0] % 8
            psi[0] += 1
            return i

        def PSK(i):
            return ("ps", i)

        pending = []

        def act(out, in_, func, R, W, bias=0.0, scale=1.0):
            return P.op("scalar", lambda e: e.activation(out=out, in_=in_, func=func, bias=bias, scale=scale), R, W)

        def vts(out, in0, s1, s2, op0, op1, R, W):
            if s2 is None:
                return P.op("vector", lambda e: e.tensor_scalar(out=out, in0=in0, scalar1=s1, scalar2=None, op0=op0), R, W)
            return P.op("vector", lambda e: e.tensor_scalar(out=out, in0=in0, scalar1=s1, scalar2=s2, op0=op0, op1=op1), R, W)

        def vtt(out, in0, in1, op, R, W):
            return P.op("vector", lambda e: e.tensor_tensor(out=out, in0=in0, in1=in1, op=op), R, W)

        def vstt(out, in0, scalar, in1, op0, op1, R, W):
            return P.op("vector", lambda e: e.scalar_tensor_tensor(out=out, in0=in0, scalar=scalar, in1=in1, op0=op0, op1=op1), R, W)

        def vcopy(out, in_, R, W):
            return P.op("vector", lambda e: e.tensor_copy(out=out, in_=in_), R, W)

        def mm(out, lhsT, rhs, start, stop, R, W, sig):
            return P.pe(lambda e: e.matmul(out, lhsT=lhsT, rhs=rhs, start=start, stop=stop), R, W, sig)

        def ln_stats4(xaps, keys, tag):
            mv4 = stats[:, 48:56]
            ve = stats[:, 56:60]
            base = 64 + 16 * tag
            sd = stats[:, base:base + 4]
            rstd4 = stats[:, base + 4:base + 8]
            nb4 = stats[:, base + 8:base + 12]
            for s_ in range(4):
                st = stats[:, 12 * s_:12 * s_ + 12]
                xap = xaps[s_]
                P.op("vector", lambda e, st=st, xap=xap: e.bn_stats(out=st[:, 0:6], in_=xap[:, 0:512]), [keys[s_]], [("st", s_)])
                P.op("vector", lambda e, st=st, xap=xap: e.bn_stats(out=st[:, 6:12], in_=xap[:, 512:1024]), [keys[s_]], [("st2", s_)])
                P.op("vector", lambda e, st=st, s_=s_: e.bn_aggr(out=mv4[:, 2 * s_:2 * s_ + 2], in_=st), [("st", s_), ("st2", s_)], ["mv4"])
            vts(ve, mv4[:, 1:8:2], EPS, None, ALU.add, None, ["mv4"], ["ve"])
            act(sd, ve, AF.Sqrt, ["ve"], [("sd", tag)])
            P.op("vector", lambda e: e.reciprocal(out=rstd4, in_=sd), [("sd", tag)], [("rstd", tag)])
            vstt(nb4, mv4[:, 0:8:2], -1.0, rstd4, ALU.mult, ALU.mult, ["mv4", ("rstd", tag)], [("nb", tag)])
            return rstd4, nb4

        tmpc = arena[:, XB:XB + 2048 + 128]
        P.dma("sync", "c0", tmpc[:, 0:128], ident_in, W=["tmpc0"])
        vcopy(ident, tmpc[:, 0:128], ["tmpc0"], ["ident"])
        P.dma("sync", "c1", tmpc[:, 128:128 + 2048], ppool_in.rearrange("p a b -> p (a b)"), W=["tmpc1"])
        vcopy(ppool.rearrange("p a b -> p (a b)"), tmpc[:, 128:128 + 2048], ["tmpc1"], ["ppool"])
        P.op("vector", lambda e: e.memset(ones_row, 1.0), [], ["ones_row"])
        zt = arena[:, XB + 4096:XB + 4096 + 512].bitcast(BF16)
        P.op("vector", lambda e: e.memset(zt, 0.0), [], ["zt"])
        for g in range(3):
            for pr in range(4):
                P.dma("sync", "zp", KT[g][pr, :, 0:PADS[g]], zt[:, 0:PADS[g]], R=["zt"])
            for ph in range(RATES[g]):
                P.dma("sync", "zp", VS[g][ph, 0:64, :], zt[0:64, :], R=["zt"])
        P.barrier()

        def ck(name):
            if stop == name:
                P.barrier()
                raise _Stop()

        try:
          ck("consts")
          for l in range(DEPTH):
              Tkv, Tq = REGIONS[l]
              xsrc = x_in if l == 0 else X1
              xdst = X1 if l == 0 else out_d

              am = Alloc(XB, XL)
              modrow = am.f32(6144)
              wad = [am.f32(8 * 512, 512), am.f32(8 * 512, 512)]
              ccol = am.f32(8)
              scol = am.f32(8)
              badar = am.f32(6144)
              colps = am.f32(32)
              P.dma("sync", "m_c", ccol, c_col, W=["ccol"])
              P.dma("sync", "m_b", badar[0:1, :], b_ada[l:l + 1, :], W=["badar"])
              P.dma("sync", "m_psc", psc, psc_in[l], W=["psc"])
              act(scol, ccol, AF.Silu, ["ccol"], ["scol"])
              for blk in range(12):
                  wb = wad[blk % 2]
                  wk = ("wad", blk % 2)
                  P.dma("sync", f"m_w{blk % 2}", wb,
                        w_ada[l, :, blk * 512:(blk + 1) * 512].rearrange("(k p) n -> p k n", p=128), W=[wk])
                  b = bank()
                  for kc in range(8):
                      mm(psb[b][0:1, :], scol[:, kc:kc + 1], wb[:, kc, :], kc == 0, kc == 7,
                         [wk, "scol"], [PSK(b)], kc == 7)
                  vtt(modrow[0:1, blk * 512:(blk + 1) * 512], psb[b][0:1, :], badar[0:1, blk * 512:(blk + 1) * 512],
                      ALU.add, [PSK(b), "badar"], ["modrow"])
              for gi, (goff, Gt) in enumerate(((2048, G1), (5120, G2))):
                  vts(modrow[0:1, goff:goff + 1024], modrow[0:1, goff:goff + 1024], 1.0, None, ALU.add, ALU.bypass,
                      ["modrow"], ["modrow"])
                  for half in range(2):
                      b = bank()
                      mm(psb[b][:, :], ones_row[0:1, :], modrow[0:1, goff + half * 512:goff + (half + 1) * 512],
                         True, True, ["modrow", "ones_row"], [PSK(b)], True)
                      vcopy(Gt[:, half * 512:(half + 1) * 512], psb[b][:, :], [PSK(b)], [("G", gi)])
              b = bank()
              for qi, qoff in enumerate((0, 1024, 3072, 4096)):
                  for kc in range(8):
                      last = (qi == 3 and kc == 7)
                      mm(psb[b][:, qi * 8 + kc:qi * 8 + kc + 1], modrow[0:1, qoff + kc * 128:qoff + (kc + 1) * 128],
                         ones_row[0:1, 0:1], True, True, ["modrow", "ones_row"], [PSK(b)], last)
              vcopy(colps, psb[b][:, 0:32], [PSK(b)], ["colps"])
              vcopy(cols[:, 0:8], colps[:, 0:8], ["colps"], ["cols"])
              vts(cols[:, 8:16], colps[:, 8:16], 1.0, None, ALU.add, ALU.bypass, ["colps"], ["cols"])
              vcopy(cols[:, 16:24], colps[:, 16:24], ["colps"], ["cols"])
              vts(cols[:, 24:32], colps[:, 24:32], 1.0, None, ALU.add, ALU.bypass, ["colps"], ["cols"])
              if debug and l == 0:
                  P.dma("sync", "dbgmod", dbg_mod[:, 0:32], cols, R=["cols"])
                  P.dma("sync", "dbgmod", dbg_mod[:, 32:1056], G1, R=[("G", 0)])
                  P.dma("sync", "dbgmod", dbg_mod[:, 1056:2080], G2, R=[("G", 1)])
              P.barrier()
              ck(f"M{l}")

              aa = Alloc(XB, XL)
              WA = aa.bf(8 * 5120, 5120)
              xA = [aa.f32(4 * 1024, 1024) for _ in range(2)]
              xn = aa.bf(4 * 1024, 1024)
              hTb = [aa.bf(8 * 512, 512) for _ in range(2)]
              stg = [aa.bf(8 * 512, 512) for _ in range(2)]
              vst = [aa.bf(8 * 128, 128) for _ in range(2)]
              ust = [aa.bf(512) for _ in range(2)]
              for kc in range(8):
                  P.dma("gpsimd", "wa", WA[:, kc, :], w_in[l, kc * 128:(kc + 1) * 128, 0:5120], W=["WA"])
              for i in range(2):
                  P.op("vector", lambda e, i=i: e.memset(vst[i][:, :, 64:128], 1.0), [], [("vst", i)])
              evq = [0]
              vsi = [0]
              usi = [0]

              def evac(out, in_, R, W, scale=None):
                  evq[0] += 1
                  if evq[0] % 2 == 0:
                      if scale is None:
                          return vcopy(out, in_, R, W)
                      return vts(out, in_, scale, None, ALU.mult, ALU.bypass, R, W)
                  return act(out, in_, AF.Identity, R, W, scale=(1.0 if scale is None else scale))

              def a_ln(j):
                  t0 = 512 * j
                  hT = hTb[j % 2]
                  hb = j % 2
                  xa = xA[j % 2]
                  kx = ("xA", j % 2)
                  P.dma("sync", f"xA{j % 2}", xa, xsrc[t0:t0 + 512, :].rearrange("(s p) f -> p s f", p=128), W=[kx])
                  rstd4, nb4 = ln_stats4([xa[:, s, :] for s in range(4)], [kx] * 4, 0)
                  for s in range(4):
                      act(xn[:, s, :], xa[:, s, :], AF.Identity, [kx, ("rstd", 0), ("nb", 0)], [("xn", s)],
                          bias=nb4[:, s:s + 1], scale=rstd4[:, s:s + 1])
                  for kp in range(4):
                      b = bank()
                      pb = psb[b][:, :].bitcast(BF16)
                      for kk in range(2):
                          kc = 2 * kp + kk
                          for s in range(4):
                              P.pe(lambda e, o=pb[:, kk * 512 + s * 128:kk * 512 + (s + 1) * 128],
                                   i=xn[:, s, kc * 128:(kc + 1) * 128]: e.transpose(out=o, in_=i, identity=ident),
                                   [("xn", s), "ident"], [PSK(b)], sig=(kk == 1 and s == 3))
                      for kk in range(2):
                          kc = 2 * kp + kk
                          src = pb[:, kk * 512:(kk + 1) * 512]
                          if kp % 2 == 0:
                              vts(hT[:, kc, :], src, cols[:, 8 + kc:9 + kc], cols[:, kc:kc + 1], ALU.mult, ALU.add,
                                  [PSK(b), "cols"], [("hT", hb, kc)])
                          else:
                              act(hT[:, kc, :], src, AF.Identity, [PSK(b), "cols"], [("hT", hb, kc)],
                                  bias=cols[:, kc:kc + 1], scale=cols[:, 8 + kc:9 + kc])
                  hTk = [("hT", hb, kc) for kc in range(8)]
                  P.dma("sync", "hTst", HT[:, :, t0:t0 + 512].rearrange("k p t -> p k t"), hT, R=hTk)

              def a_qk(j):
                  t0 = 512 * j
                  hT = hTb[j % 2]
                  hb = j % 2
                  need_q = (t0 < Tq)
                  for g in range(3):
                      sg = stg[g % 2]
                      ks = ("stg", g % 2)
                      for which in ((0, 1) if need_q else (1,)):
                          for pr in range(4):
                              col0 = g * 1536 + which * 512 + pr * 128
                              b = bank()
                              for kc in range(8):
                                  mm(psb[b][:, :], WA[:, kc, col0:col0 + 128], hT[:, kc, :], kc == 0, kc == 7,
                                     ["WA", ("hT", hb, kc)], [PSK(b)], kc == 7)
                              evac(sg[:, which * 4 + pr, :], psb[b][:, :], [PSK(b)], [(ks, which * 4 + pr)],
                                   scale=(0.125 if which == 0 else None))
                      if need_q:
                          P.dma("sync", f"stg{g % 2}", QT[g][:, :, t0:t0 + 512].rearrange("a p t -> p a t"), sg[:, 0:4, :], R=[(ks, i_) for i_ in range(4)])
                      P.dma("sync", f"stg{g % 2}", KT[g][:, :, PADS[g] + t0:PADS[g] + t0 + 512].rearrange("a p t -> p a t"),
                            sg[:, 4:8, :], R=[(ks, i_) for i_ in range(4, 8)])

              def a_vu(j):
                  t0 = 512 * j
                  hT = hTb[j % 2]
                  hb = j % 2
                  for g in range(3):
                      r = RATES[g]
                      for blk in range(4):
                          if g == 0:
                              lcols = lambda kc, blk=blk: hT[:, kc, blk * 128:(blk + 1) * 128]
                          elif g == 1:
                              lcols = lambda kc, blk=blk: hT[:, kc, :].rearrange("p (n r) -> p r n", r=4)[:, blk, :]
                          else:
                              lcols = lambda kc, blk=blk: hT[:, kc, :].rearrange("p (n r) -> p r n", r=4)[:, blk, :]
                          b = bank()
                          c0 = g * 1536 + 1024
                          for kc in range(8):
                              mm(psb[b][:, :], lcols(kc), WA[:, kc, c0:c0 + 512], kc == 0, kc == 7,
                                 ["WA", ("hT", hb, kc)], [PSK(b)], kc == 7)
                          vi = vsi[0] % 2
                          vsi[0] += 1
                          evac(vst[vi][:, :, 0:64], psb[b][:, :].rearrange("p (a b) -> p a b", b=64), [PSK(b)], [("vst", vi)])
                          vflat = vst[vi].rearrange("p a b -> p (a b)")
                          if g == 0:
                              P.dma("sync", f"vst{vi}", VS[0][0, 64 + t0 + blk * 128:64 + t0 + (blk + 1) * 128, :], vflat, R=[("vst", vi)])
                          elif g == 1:
                              P.dma("sync", f"vst{vi}", VS[1][blk, 64 + t0 // 4:64 + t0 // 4 + 128, :], vflat, R=[("vst", vi)])
                          else:
                              for pl in range(4):
                                  P.dma("sync", f"vst{vi}", VS[2][blk + 4 * pl, 64 + t0 // 16:64 + t0 // 16 + 32, :],
                                        vflat[pl::4, :], R=[("vst", vi)])
                  for s in range(4):
                      if t0 + s * 128 >= Tq + 128:
                          continue
                      b = bank()
                      for kc in range(8):
                          mm(psb[b][:, :], hT[:, kc, s * 128:(s + 1) * 128], WA[:, kc, 4608:5120], kc == 0, kc == 7,
                             ["WA", ("hT", hb, kc)], [PSK(b)], kc == 7)
                      ui = usi[0] % 2
                      usi[0] += 1
                      evac(ust[ui], psb[b][:, :], [PSK(b)], [("ust", ui)])
                      P.dma("sync", f"ust{ui}", US[t0 + s * 128:t0 + (s + 1) * 128, :], ust[ui], R=[("ust", ui)])

              nta = Tkv // 512
              a_ln(0)
              for j in range(nta):
                  a_qk(j)
                  if j + 1 < nta:
                      a_ln(j + 1)
                  a_vu(j)
              P.barrier()
              ck(f"A{l}")

              ab = Alloc(XB, XL)
              attT = ab.bf(4 * 2048, 2048)
              BC0 = ab.o
              maskb = ab.bf(12 * 512, 512)
              QTs = [ab.bf(2048) for _ in range(3)]
              KTs = [ab.bf(2048 + 2 * PADS[g]) for g in range(3)]
              NVC = 17 + 20 + 32
              Vc = ab.bf(NVC * 256, 256)
              acc = ab.f32(2 * 2048, 2048)
              Rr = ab.f32(2 * 2048, 2048)
              tmpE = [ab.bf(1024) for _ in range(4)]
              Ep = [ab.bf(1024) for _ in range(4)]
              ac = Alloc(BC0, XL)
              lnr = [ac.f32(1024) for _ in range(4)]
              hTc = ac.bf(8 * 512, 512)
              uT = ac.bf(6 * 512, 512)
              pooledT = ac.bf(4 * 512, 512)
              pmT = ac.bf(4 * 512, 512)
              Wmix = ac.bf(4 * 128, 128)
              sg_ = [ac.f32(512), ac.f32(512)]
              t12 = [ac.f32(512), ac.f32(512)]
              mergedT = ac.bf(8 * 512, 512)
              xmid2 = [ac.f32(4 * 1024, 1024) for _ in range(2)]
              xs = ac.f32(1024)
              xnf = ac.f32(1024)
              xn2 = ac.bf(4 * 1024, 1024)
              h2T = ac.bf(8 * 512, 512)
              gT = ac.bf(22 * 512, 512)
              sa = [ac.f32(512), ac.f32(512)]
              tg = ac.f32(512)
              ring = [ac.bf(8 * 512, 512) for _ in range(4)]
              ringi = [0]

              def wload(src2d, nk, ncol):
                  i = ringi[0] % 4
                  ringi[0] += 1
                  slot = ring[i]
                  P.dma("gpsimd", f"ring{i}", slot[:, 0:nk, 0:ncol], src2d.rearrange("(k p) n -> p k n", p=128),
                        W=[("ring", i)])
                  return slot, ("ring", i)

              nsb = (Tq + 2047) // 2048
              for M in range(nsb):
                  sb0 = 2048 * M
                  nq = min(2048, Tq - sb0)
                  if M > 0:
                      P.barrier()
                  P.dma("gpsimd", "maskb", maskb, mask_in, W=["maskb"])
                  for pr in range(4):
                      vidx = {}
                      ci = 0
                      for g in range(3):
                          P.dma("sync", f"qts{g}", QTs[g][:, 0:nq], QT[g][pr, :, sb0:sb0 + nq], W=[("QTs", g)])
                          P.dma("sync", f"kts{g}", KTs[g][:, 0:nq + 2 * PADS[g]], KT[g][pr, :, sb0:sb0 + nq + 2 * PADS[g]],
                                W=[("KTs", g)])
                          r = RATES[g]
                          npos = nq // r
                          nqb = min(128, npos)
                          nQb = npos // nqb
                          for ph in range(r):
                              for m in range(nQb + 1):
                                  nk = 128 if m < nQb else nqb
                                  row0 = sb0 // r + 128 * m
                                  P.dma("sync", f"vc{g}", Vc[0:nk, ci, :], VS[g][ph, row0:row0 + nk, pr * 256:(pr + 1) * 256],
                                        W=[("Vc", g)])
                                  vidx[(g, ph, m)] = ci
                                  ci += 1
                      ck("Bloads")
                      units = []
                      for g in range(3):
                          r = RATES[g]
                          npos = nq // r
                          nqb = min(128, npos)
                          nQb = npos // nqb
                          for ph in range(r):
                              for m in range(nQb):
                                  units.append((g, r, nqb, ph, m))
                      LOOK = 2
                      assert len(units) % 2 == 0
                      upairs = [(units[2 * i], units[2 * i + 1]) for i in range(len(units) // 2)]

                      def emit_S(idx):
                          bS = [(2 * idx) % 6, (2 * idx + 1) % 6]
                          ui = idx % 4
                          g = upairs[idx][0][0]
                          kq = [("QTs", g), ("KTs", g)]
                          for sl in range(2):
                              lo, hi = 64 * sl, 64 * sl + 64
                              ps = psb[bS[sl]]
                              for j in range(2):
                                  g_, r, nqb, ph, m = upairs[idx][j]
                                  qs = QTs[g][:, r * 128 * m + ph: r * 128 * m + ph + r * (nqb - 1) + 1: r]
                                  kA = KTs[g][:, r * 128 * m + ph: r * 128 * m + ph + r * 127 + 1: r]
                                  kB = KTs[g][:, r * 128 * (m + 1) + ph: r * 128 * (m + 1) + ph + r * (nqb - 1) + 1: r]
                                  c0 = 256 * j
                                  mm(ps[:, c0:c0 + nqb], kA[lo:hi, :], qs[lo:hi, :], True, True, kq, [PSK(bS[sl])], False)
                                  mm(ps[0:nqb, c0 + 128:c0 + 128 + nqb], kB[lo:hi, :], qs[lo:hi, :], True, True,
                                     kq, [PSK(bS[sl])], j == 1)
                              act(tmpE[ui].rearrange("p (j s c) -> p j s c", j=2, s=2)[:, :, sl, :],
                                  ps[:, :].rearrange("p (j c) -> p j c", j=2), AF.Exp,
                                  [PSK(bS[sl])], [("tmpE", ui, sl)])
                          for j in range(2):
                              vtt(Ep[ui][:, 512 * j:512 * (j + 1)], tmpE[ui][:, 512 * j:512 * (j + 1)], maskb[:, g * 4 + pr, :], ALU.mult,
                                  [("tmpE", ui, 0), ("tmpE", ui, 1), "maskb"], [("Ep", ui, j)])

                      def emit_PV(idx):
                          ui = idx % 4
                          b2 = 6 + idx % 2
                          ps2 = psb[b2]
                          for j in range(2):
                              g, r, nqb, ph, m = upairs[idx][j]
                              cA = vidx[(g, ph, m)]
                              cB = vidx[(g, ph, m + 1)]
                              e0 = 512 * j
                              for sl in range(2):
                                  o = ps2[:, j * 256 + sl * 128:j * 256 + sl * 128 + nqb]
                                  mm(o, Vc[:, cA, sl * 128:(sl + 1) * 128],
                                     Ep[ui][:, e0 + sl * 256:e0 + sl * 256 + nqb], True, False, [("Vc", g), ("Ep", ui, j)], [PSK(b2)], False)
                                  mm(o, Vc[0:nqb, cB, sl * 128:(sl + 1) * 128],
                                     Ep[ui][0:nqb, e0 + sl * 256 + 128:e0 + sl * 256 + 128 + nqb], False, True,
                                     [("Vc", g), ("Ep", ui, j)], [PSK(b2)], (j == 1 and sl == 1))
                          for j in range(2):
                              g, r, nqb, ph, m = upairs[idx][j]
                              acv = acc[:, :, r * 128 * m + ph: r * 128 * m + ph + r * (nqb - 1) + 1: r]
                              pv = ps2[:, j * 256:(j + 1) * 256].rearrange("p (a b) -> p a b", b=128)[:, :, 0:nqb]
                              if g == 0:
                                  vcopy(acv, pv, [PSK(b2)], ["acc"])
                              else:
                                  vtt(acv, acv, pv, ALU.add, [PSK(b2), "acc"], ["acc"])

                      for idx in range(len(upairs) + LOOK):
                          if idx < len(upairs):
                              emit_S(idx)
                          if idx - LOOK >= 0:
                              emit_PV(idx - LOOK)
                      act(Rr[0:64, :, 0:nq], acc[64:128, :, 0:nq], AF.Ln, ["acc"], ["Rr"])
                      act(Rr[0:64, :, 0:nq], Rr[0:64, :, 0:nq], AF.Exp, ["Rr"], ["Rr"], scale=-1.0)
                      vtt(attT[0:64, pr, 0:nq], acc[0:64, 0, 0:nq], Rr[0:64, 0, 0:nq], ALU.mult, ["acc", "Rr"], ["attT"])
                      vtt(attT[64:128, pr, 0:nq], acc[0:64, 1, 0:nq], Rr[0:64, 1, 0:nq], ALU.mult, ["acc", "Rr"], ["attT"])
                      if debug and l == 0:
                          P.dma("sync", "dbgatt", dbg_att[pr, :, sb0:sb0 + nq], attT[:, pr, 0:nq], R=["attT"])
                      ck("Bn")

                  P.barrier()
                  ck(f"B{l}_{M}")
                  P.dma("gpsimd", "wmix", Wmix, w_mix[l].rearrange("g c d -> c g d"), W=["Wmix"])
                  for i_, src_ in enumerate((ln1g, ln1b, ln2g, ln2b)):
                      P.dma("sync", f"lnr{i_}", lnr[i_], src_[l:l + 1, :].partition_broadcast(128).rearrange("p a f -> p (a f)"),
                            W=[("lnr", i_)])

                  def drain(n):
                      while n > 0 and pending:
                          pending.pop(0)()
                          n -= 1

                  def c_front(tt):
                      t0 = sb0 + 512 * tt
                      tl0 = 512 * tt
                      tb = tt % 2
                      xm = xmid2[tb]
                      P.dma("sync", "hTc", hTc, HT[:, :, t0:t0 + 512].rearrange("k p t -> p k t"), W=["hTc"])
                      if t0 == 0:
                          P.dma("sync", "uT", uT[:, 1:6, :], US[0:640, :].rearrange("(s p) f -> p s f", p=128), W=["uT"])
                      else:
                          P.dma("sync", "uT", uT, US[t0 - 128:t0 + 640, :].rearrange("(s p) f -> p s f", p=128), W=["uT"])
                      for g in range(4):
                          b = bank()
                          for bl in range(4):
                              first = (t0 == 0 and bl == 0)
                              kind = 3 if first else 0
                              o = psb[b][:, bl * 128:(bl + 1) * 128]
                              mm(o, uT[:, bl + 1, g * 128:(g + 1) * 128], ppool[:, g * 4 + kind, :], True, False,
                                 ["uT", "ppool"], [PSK(b)], False)
                              if not first:
                                  mm(o, uT[:, bl, g * 128:(g + 1) * 128], ppool[:, g * 4 + 1, :], False, False,
                                     ["uT", "ppool"], [PSK(b)], False)
                              mm(o, uT[:, bl + 2, g * 128:(g + 1) * 128], ppool[:, g * 4 + 2, :], False, True,
                                 ["uT", "ppool"], [PSK(b)], bl == 3)
                          act(pooledT[:, g, :], psb[b][:, :], AF.Identity, [PSK(b)], [("pooledT", g)])
                          b2 = bank()
                          mm(psb[b2][:, :], Wmix[:, g, :], pooledT[:, g, :], True, True, ["Wmix", ("pooledT", g)], [PSK(b2)], True)
                          vts(pmT[:, g, :], psb[b2][:, :], psc[:, g:g + 1], None, ALU.mult, ALU.bypass, [PSK(b2), "psc"], [("pmT", g)])
                      pmk = [("pmT", g) for g in range(4)]
                      for jb in range(2):
                          Wga, kga = wload(w_in[l, :, 5120 + 512 * jb:5120 + 512 * (jb + 1)], 8, 512)
                          Wgp, kgp = wload(w_in[l, :, 6144 + 512 * jb:6144 + 512 * (jb + 1)], 8, 512)
                          Wao, kao = wload(w_ao[l, :, 512 * jb:512 * (jb + 1)], 4, 512)
                          Wpo, kpo = wload(w_po[l, :, 512 * jb:512 * (jb + 1)], 4, 512)
                          for cc in range(4):
                              c = 4 * jb + cc
                              csl = slice(cc * 128, (cc + 1) * 128)
                              bga, bgp, bya, byp = bank(), bank(), bank(), bank()
                              for kc in range(8):
                                  mm(psb[bga][:, :], Wga[:, kc, csl], hTc[:, kc, :], kc == 0, kc == 7, [kga, "hTc"], [PSK(bga)], kc == 7)
                              for kc in range(8):
                                  mm(psb[bgp][:, :], Wgp[:, kc, csl], hTc[:, kc, :], kc == 0, kc == 7, [kgp, "hTc"], [PSK(bgp)], kc == 7)
                              for kc in range(4):
                                  mm(psb[bya][:, :], Wao[:, kc, csl], attT[:, kc, tl0:tl0 + 512], kc == 0, kc == 3, [kao, "attT"], [PSK(bya)], kc == 3)
                              for kc in range(4):
                                  mm(psb[byp][:, :], Wpo[:, kc, csl], pmT[:, kc, :], kc == 0, kc == 3, [kpo] + pmk, [PSK(byp)], kc == 3)
                              act(sg_[0], psb[bga][:, :], AF.Sigmoid, [PSK(bga)], [("sg", 0)])
                              act(sg_[1], psb[bgp][:, :], AF.Sigmoid, [PSK(bgp)], [("sg", 1)])
                              vtt(t12[0], sg_[0], psb[bya][:, :], ALU.mult, [("sg", 0), PSK(bya)], [("t12", 0)])
                              vtt(t12[1], sg_[1], psb[byp][:, :], ALU.mult, [("sg", 1), PSK(byp)], [("t12", 1)])
                              vtt(mergedT[:, c, :], t12[0], t12[1], ALU.add, [("t12", 0), ("t12", 1)], [("mergedT", c)])
                              drain(1)
                      mk = [("mergedT", c) for c in range(8)]
                      if debug and l == 0:
                          P.dma("sync", "dbgmrg", dbg_mrg[:, :, t0:t0 + 512].rearrange("k p t -> p k t"), mergedT, R=mk)
                      for half in range(2):
                          Wo, kwo = wload(w_o[l, :, 512 * half:512 * (half + 1)], 8, 512)
                          for s in range(4):
                              b = bank()
                              for kc in range(8):
                                  mm(psb[b][:, :], mergedT[:, kc, s * 128:(s + 1) * 128], Wo[:, kc, :], kc == 0, kc == 7,
                                     [kwo] + mk, [PSK(b)], kc == 7)
                              vtt(xm[:, s, half * 512:(half + 1) * 512], psb[b][:, :], G1[:, half * 512:(half + 1) * 512],
                                  ALU.mult, [PSK(b), ("G", 0)], [("xmid", tb, s)])

                  def c_ln1(tt):
                      t0 = sb0 + 512 * tt
                      tb = tt % 2
                      xm = xmid2[tb]
                      xmk = [("xmid", tb, s) for s in range(4)]
                      st_ = {}
                      pieces = []

                      def p_res(s):
                          P.dma("sync", "xs", xs, xsrc[t0 + s * 128:t0 + (s + 1) * 128, :], W=["xs"])
                          vstt(xm[:, s, :], xs, ALPHA, xm[:, s, :], ALU.mult, ALU.add, ["xs", ("xmid", tb, s)], [("xmid", tb, s)])

                      def p_stats1():
                          st_["a"] = ln_stats4([xm[:, s, :] for s in range(4)], xmk, 1)

                      def p_aff(s):
                          rstd4, nb4 = st_["a"]
                          act(xnf, xm[:, s, :], AF.Identity, [("xmid", tb, s), ("rstd", 1), ("nb", 1)], ["xnf"],
                              bias=nb4[:, s:s + 1], scale=rstd4[:, s:s + 1])
                          vtt(xm[:, s, :], xnf, lnr[0], ALU.mult, ["xnf", ("lnr", 0)], [("xmid", tb, s)])
                          vtt(xm[:, s, :], xm[:, s, :], lnr[1], ALU.add, [("xmid", tb, s), ("lnr", 1)], [("xmid", tb, s)])

                      def p_stats2():
                          if debug and l == 0:
                              P.dma("sync", "dbgxmid", dbg_xmid[t0:t0 + 512, :].rearrange("(s p) f -> p s f", p=128), xm, R=xmk)
                          st_["b"] = ln_stats4([xm[:, s, :] for s in range(4)], xmk, 2)

                      def p_xn2(s):
                          rstd4, nb4 = st_["b"]
                          act(xn2[:, s, :], xm[:, s, :], AF.Identity, [("xmid", tb, s), ("rstd", 2), ("nb", 2)], [("xn2", s)],
                              bias=nb4[:, s:s + 1], scale=rstd4[:, s:s + 1])
                      for s in range(4):
                          pieces.append(lambda s=s: p_res(s))
                      pieces.append(p_stats1)
                      for s in range(4):
                          pieces.append(lambda s=s: p_aff(s))
                      pieces.append(p_stats2)
                      for s in range(4):
                          pieces.append(lambda s=s: p_xn2(s))
                      return pieces

                  def c_back(tt):
                      tb = tt % 2
                      xm = xmid2[tb]
                      for kp in range(4):
                          b = bank()
                          pb = psb[b][:, :].bitcast(BF16)
                          for kk in range(2):
                              kc = 2 * kp + kk
                              for s in range(4):
                                  P.pe(lambda e, o=pb[:, kk * 512 + s * 128:kk * 512 + (s + 1) * 128],
                                       i=xn2[:, s, kc * 128:(kc + 1) * 128]: e.transpose(out=o, in_=i, identity=ident),
                                       [("xn2", s), "ident"], [PSK(b)], sig=(kk == 1 and s == 3))
                          for kk in range(2):
                              kc = 2 * kp + kk
                              src = pb[:, kk * 512:(kk + 1) * 512]
                              if kp % 2 == 0:
                                  vts(h2T[:, kc, :], src, cols[:, 24 + kc:25 + kc], cols[:, 16 + kc:17 + kc], ALU.mult, ALU.add,
                                      [PSK(b), "cols"], [("h2T", kc)])
                              else:
                                  act(h2T[:, kc, :], src, AF.Identity, [PSK(b), "cols"], [("h2T", kc)],
                                      bias=cols[:, 16 + kc:17 + kc], scale=cols[:, 24 + kc:25 + kc])
                      h2k = [("h2T", kc) for kc in range(8)]
                      for jb in range(6):
                          bw = 512 if jb < 5 else 256
                          Wa, kwa = wload(w_fi[l, :, 512 * jb:512 * jb + bw], 8, bw)
                          Wb, kwb = wload(w_fi[l, :, DFF + 512 * jb:DFF + 512 * jb + bw], 8, bw)
                          for cc in range(bw // 128):
                              c = 4 * jb + cc
                              csl = slice(cc * 128, (cc + 1) * 128)
                              ba, bb_ = bank(), bank()
                              for kc in range(8):
                                  mm(psb[ba][:, :], Wa[:, kc, csl], h2T[:, kc, :], kc == 0, kc == 7, [kwa] + h2k, [PSK(ba)], kc == 7)
                              for kc in range(8):
                                  mm(psb[bb_][:, :], Wb[:, kc, csl], h2T[:, kc, :], kc == 0, kc == 7, [kwb] + h2k, [PSK(bb_)], kc == 7)
                              si = c % 2
                              act(sa[si], psb[ba][:, :], AF.Silu, [PSK(ba)], [("sa", si)])
                              vtt(gT[:, c, :], sa[si], psb[bb_][:, :], ALU.mult, [("sa", si), PSK(bb_)], [("gT", c)])
                              drain(1)
                      gk = [("gT", c) for c in range(22)]
                      for half in range(2):
                          bF = [bank() for _ in range(4)]
                          for kg, (k0, nk) in enumerate(((0, 8), (8, 8), (16, 6))):
                              Wf, kwf = wload(w_fo[l, k0 * 128:(k0 + nk) * 128, 512 * half:512 * (half + 1)], nk, 512)
                              for s in range(4):
                                  for kk in range(nk):
                                      kc = k0 + kk
                                      mm(psb[bF[s]][:, :], gT[:, kc, s * 128:(s + 1) * 128], Wf[:, kk, :], kc == 0, kc == 21,
                                         [kwf] + gk, [PSK(bF[s])], kk == nk - 1)
                          for s in range(4):
                              hs = slice(half * 512, (half + 1) * 512)
                              vtt(tg, psb[bF[s]][:, :], G2[:, hs], ALU.mult, [PSK(bF[s]), ("G", 1)], ["tg"])
                              vstt(xm[:, s, hs], xm[:, s, hs], ALPHA, tg, ALU.mult, ALU.add, ["tg", ("xmid", tb, s)], [("xmid", tb, s)])

                  def c_ln2(tt):
                      t0 = sb0 + 512 * tt
                      tb = tt % 2
                      xm = xmid2[tb]
                      xmk = [("xmid", tb, s) for s in range(4)]
                      st_ = {}

                      def p_stats():
                          st_["a"] = ln_stats4([xm[:, s, :] for s in range(4)], xmk, 3)

                      def p_fin(s):
                          rstd4, nb4 = st_["a"]
                          act(xnf, xm[:, s, :], AF.Identity, [("xmid", tb, s), ("rstd", 3), ("nb", 3)], ["xnf"],
                              bias=nb4[:, s:s + 1], scale=rstd4[:, s:s + 1])
                          vtt(xm[:, s, :], xnf, lnr[2], ALU.mult, ["xnf", ("lnr", 2)], [("xmid", tb, s)])
                          vtt(xm[:, s, :], xm[:, s, :], lnr[3], ALU.add, [("xmid", tb, s), ("lnr", 3)], [("xmid", tb, s)])
                          r0 = t0 + s * 128
                          P.dma("sync", f"xo{s % 2}", xdst[r0:r0 + 128, :], xm[:, s, :], R=[("xmid", tb, s)])
                          if debug and l == 0:
                              P.dma("sync", f"xo{s % 2}", dbg_d[r0:r0 + 128, :], xm[:, s, :], R=[("xmid", tb, s)])
                      return [p_stats] + [(lambda s=s: p_fin(s)) for s in range(4)]

                  ntile = nq // 512
                  pending.clear()
                  c_front(0)
                  pending.extend(c_ln1(0))
                  if ntile == 1:
                      drain(99)
                  for tt in range(ntile):
                      if tt + 1 < ntile:
                          c_front(tt + 1)
                      drain(99)
                      if tt + 1 < ntile:
                          pending.extend(c_ln1(tt + 1))
                      c_back(tt)
                      drain(99)
                      pending.extend(c_ln2(tt))
                  drain(99)
              P.barrier()
        except _Stop:
            pass

        P.emit(nc, stack)
    return nc, P


_CACHE = {}


def _consts(flip):
    n = 24
    slopes = np.exp2(-8.0 * np.arange(1, n + 1, dtype=np.float64) / n).reshape(3, 8)
    kk = np.arange(128)[:, None]
    qq = np.arange(128)[None, :]
    mask = np.zeros((128, 12, 512), np.float32)
    for g in range(3):
        r = RATES[g]
        for pr in range(4):
            for sl in range(2):
                s = slopes[g, 2 * pr + sl]
                for ab, rel in enumerate((kk - 64 - qq, kk + 64 - qq)):
                    m = np.where(np.abs(rel) <= 64, -s * r * np.abs(rel), NEG)
                    mask[:, g * 4 + pr, sl * 256 + ab * 128:sl * 256 + (ab + 1) * 128] = m
    pp = np.zeros((128, 16, 128), np.float32)
    for g, w in enumerate((2, 4, 8, 16)):
        h = w // 2
        lo_off, hi_off = (-h, h - 1) if not flip else (-h + 1, h)
        for t in range(128):
            for kind in range(4):
                for sabs in range(t + lo_off, t + hi_off + 1):
                    if kind in (0, 3):
                        s_loc = sabs
                    elif kind == 1:
                        s_loc = sabs + 128
                    else:
                        s_loc = sabs - 128
                    if 0 <= s_loc < 128:
                        if kind == 3:
                            lo = max(t + lo_off, 0)
                            cnt = t + hi_off - lo + 1
                        else:
                            cnt = w
                        pp[s_loc, g * 4 + kind, t] += 1.0 / cnt
                if kind in (0, 3):
                    pp[t, g * 4 + kind, t] -= 1.0
    ident = np.eye(128, dtype=np.float32)
    return mask, pp, ident


def kernel(x, c, w_ada, b_ada, w_in, w_pool_mix, pool_scale, w_att_out, w_pool_out, w_o,
           ln1_g, ln1_b, w_ffn_in, w_ffn_out, ln2_g, ln2_b, _debug=False):
    f = lambda a: np.ascontiguousarray(np.asarray(a, dtype=np.float32))
    x = f(x)
    c = f(c)
    shared = dict(w_ada=f(w_ada), b_ada=f(b_ada), w_in=f(w_in), w_pool_mix=f(w_pool_mix),
                  w_att_out=f(w_att_out), w_pool_out=f(w_pool_out), w_o=f(w_o), ln1_g=f(ln1_g), ln1_b=f(ln1_b),
                  w_ffn_in=f(w_ffn_in), w_ffn_out=f(w_ffn_out), ln2_g=f(ln2_g), ln2_b=f(ln2_b))
    ps = f(pool_scale)
    shared["pool_scale_col"] = np.ascontiguousarray(ps.reshape(DEPTH, 4, 128).transpose(0, 2, 1))
    key = "dbg" if _debug else "main"
    if key not in _CACHE:
        _CACHE[key] = build_program(debug=_debug)[0]
    nc = _CACHE[key]
    in_maps = []
    for core in range(8):
        b, half = core // 2, core % 2
        if half == 0:
            xl = x[b, 0:TL]
        else:
            xl = x[b, SEQ - TL:SEQ][::-1]
        mask, pp, ident = _consts(flip=(half == 1))
        m = dict(shared)
        m["x"] = np.ascontiguousarray(xl)
        m["c_col"] = np.ascontiguousarray(c[b].reshape(8, 128).T)
        m["maskbias"] = np.where(mask <= NEG / 2, 0.0, np.exp(mask.astype(np.float64))).astype(np.float32)
        m["ppool"] = pp
        m["ident"] = ident
        in_maps.append(m)
    if isinstance(_debug, int) and not isinstance(_debug, bool):
        res = run_bass_kernel_spmd(nc, [in_maps[_debug]], core_ids=[0])
        return res.results[0]
    res = run_bass_kernel_spmd(nc, in_maps, core_ids=list(range(8)))
    out = np.empty((BATCH, SEQ, D), np.float32)
    for core in range(8):
        b, half = core // 2, core % 2
        o = np.asarray(res.results[core]["out"], dtype=np.float32)
        if half == 0:
            out[b, 0:TOWN] = o
        else:
            out[b, TOWN:SEQ] = o[::-1]
    if _debug:
        return out, [np.asarray(r["dbg"]) for r in res.results]
    return out
```
